# Optimizing a Trainium2 kernel written in Bass

```python
import math
import jax, jax.numpy as jnp
from jax import lax
import numpy as np

D_MODEL = 1024
BATCH = 8
SEQ = 2048
DEPTH = 2
DEC_BATCH = 128
DEC_SEQ = 1
PAST_LEN = 16384
PAGE_SIZE = 128

N_BR = 5
W_BR = D_MODEL // 2
N_HEADS = 4
HEAD_DIM = W_BR // N_HEADS
CONV_W = 4
N_MEM = 256
D_FF = -(-8 * D_MODEL // (3 * 256)) * 256
LRU_C = 8.0
GDN_CHUNK = 64
HGRN_CHUNK = 16
RET_CHUNK = 64
ROPE_BASE = 10000.0
LN_EPS = 1e-5
NORM_EPS = 1e-6
DEEPNORM_ALPHA = (2 * DEPTH) ** 0.25
DEEPNORM_BETA = (8 * DEPTH) ** -0.25
IN_SPLITS = (W_BR, 3 * W_BR, W_BR, N_HEADS, N_HEADS, W_BR, W_BR, W_BR, W_BR, W_BR, W_BR, W_BR, W_BR, W_BR)
N_IN = sum(IN_SPLITS)

kernel_name = "hybrid_gated_parallel_recurrent_decoder_step"


def _split(a, sizes):
    idx = np.cumsum(sizes)[:-1].tolist()
    return jnp.split(a, idx, axis=-1)


def _heads(a):
    return a.reshape(a.shape[:-1] + (N_HEADS, HEAD_DIM))


def _layernorm(x, g, b):
    xf = x.astype(jnp.float32)
    mu = jnp.mean(xf, -1, keepdims=True)
    var = jnp.mean(jnp.square(xf - mu), -1, keepdims=True)
    return ((xf - mu) * lax.rsqrt(var + LN_EPS) * g + b).astype(x.dtype)


def _head_rmsnorm(o, w):
    of = o.astype(jnp.float32)
    of = of * lax.rsqrt(jnp.mean(of * of, -1, keepdims=True) + NORM_EPS)
    return of.reshape(o.shape[:2] + (-1,)) * w


def _head_groupnorm(o, w, b):
    of = o.astype(jnp.float32)
    mu = jnp.mean(of, -1, keepdims=True)
    var = jnp.mean(jnp.square(of - mu), -1, keepdims=True)
    return ((of - mu) * lax.rsqrt(var + NORM_EPS)).reshape(o.shape[:2] + (-1,)) * w + b


def _l2norm(a):
    af = a.astype(jnp.float32)
    return af * lax.rsqrt(jnp.sum(af * af, -1, keepdims=True) + NORM_EPS)


def _causal_conv(x, buf, w, b=None):
    T = x.shape[1]
    xp = jnp.concatenate([buf.astype(x.dtype), x], axis=1)
    out = w[0] * xp[:, 0:T]
    for j in range(1, CONV_W):
        out = out + w[j] * xp[:, j:j + T]
    if b is not None:
        out = out + b
    return out, xp[:, T:]


def _rotary(x, pos):
    half = HEAD_DIM // 2
    inv = ROPE_BASE ** (-jnp.arange(half, dtype=jnp.float32) / half)
    ang = pos.astype(jnp.float32)[:, None] * inv
    cos, sin = jnp.cos(ang)[None, :, None, :], jnp.sin(ang)[None, :, None, :]
    xf = x.astype(jnp.float32)
    x1, x2 = xf[..., :half], xf[..., half:]
    return jnp.concatenate([x1 * cos - x2 * sin, x2 * cos + x1 * sin], axis=-1)


def _lin_comb(e1, e2):
    a1, b1 = e1
    a2, b2 = e2
    return a1 * a2, a2 * b1 + b2


def _rglru(xc, h0, wa, ba, wi, bi, lam):
    f32 = jnp.float32
    xh = _heads(xc)
    r = jax.nn.sigmoid((jnp.einsum("bthi,hij->bthj", xh, wa).reshape(xc.shape) + ba).astype(f32))
    ig = jax.nn.sigmoid((jnp.einsum("bthi,hij->bthj", xh, wi).reshape(xc.shape) + bi).astype(f32))
    log_a = -LRU_C * r * jax.nn.softplus(-lam.astype(f32))
    a = jnp.exp(log_a)
    b = jnp.sqrt(-jnp.expm1(2.0 * log_a)) * (ig * xc.astype(f32))
    b = b.at[:, 0].add(a[:, 0] * h0.astype(f32))
    _, h = lax.associative_scan(_lin_comb, (a, b), axis=1)
    return h, h[:, -1]


def _chunk_len(T, chunk):
    return chunk if T % chunk == 0 else T


def _to_chunks(a, N, C):
    B_, T, H, X = a.shape
    return a.reshape(B_, N, C, H, X).transpose(1, 0, 3, 2, 4)


def _from_chunks(o):
    N, B_, H, C, X = o.shape
    return o.transpose(1, 0, 3, 2, 4).reshape(B_, N * C, H, X)


def _chunked_gla(q, k, v, log_f, s0, chunk):
    f32 = jnp.float32
    T = q.shape[1]
    C = _chunk_len(T, chunk)
    N = T // C
    qc, kc, vc, gc = (_to_chunks(a.astype(f32), N, C) for a in (q, k, v, log_f))
    G = jnp.cumsum(gc, axis=3)
    q_g = qc * jnp.exp(G)
    k_g = kc * jnp.exp(-G)
    incl = jnp.tril(jnp.ones((C, C), bool))
    a_qk = jnp.where(incl, jnp.einsum("nbhik,nbhjk->nbhij", q_g, k_g), 0.0)
    o_intra = jnp.einsum("nbhij,nbhjv->nbhiv", a_qk, vc)
    g_last = G[:, :, :, -1]
    k_end = kc * jnp.exp(g_last[:, :, :, None] - G)

    def step(S, xs):
        qg, ke, vv, gl = xs
        o = jnp.einsum("bhck,bhkv->bhcv", qg, S)
        S = jnp.exp(gl)[..., None] * S + jnp.einsum("bhck,bhcv->bhkv", ke, vv)
        return S, o

    s_fin, o_inter = lax.scan(step, s0.astype(f32), (q_g, k_end, vc, g_last))
    return _from_chunks(o_intra + o_inter), s_fin


def _chunked_gdn(q, k, v, g, beta, s0, chunk):
    f32 = jnp.float32
    T = q.shape[1]
    V = v.shape[-1]
    C = _chunk_len(T, chunk)
    N = T // C
    qc, kc, vc = (_to_chunks(a.astype(f32), N, C) for a in (q, k, v))
    gc, bc = (_to_chunks(a.astype(f32)[..., None], N, C)[..., 0] for a in (g, beta))
    G = jnp.cumsum(gc, axis=-1)
    incl = jnp.tril(jnp.ones((C, C), bool))
    strict = jnp.tril(jnp.ones((C, C), bool), -1)
    decay = jnp.exp(jnp.where(incl, G[..., :, None] - G[..., None, :], -jnp.inf))
    kb = kc * bc[..., None]
    a_kk = jnp.where(strict, jnp.einsum("nbhik,nbhjk->nbhij", kb, kc) * decay, 0.0)
    rhs = jnp.concatenate([vc * bc[..., None], kb * jnp.exp(G)[..., None]], axis=-1)
    sol = lax.linalg.triangular_solve(a_kk + jnp.eye(C, dtype=f32), rhs, left_side=True, lower=True, unit_diagonal=True)
    u, w = sol[..., :V], sol[..., V:]
    a_qk = jnp.einsum("nbhik,nbhjk->nbhij", qc, kc) * decay
    q_g = qc * jnp.exp(G)[..., None]
    g_last = G[..., -1]
    k_end = kc * jnp.exp(g_last[..., None] - G)[..., None]

    def step(S, xs):
        qg, ke, uu, ww, aqk, gl = xs
        v_new = uu - jnp.einsum("bhck,bhkv->bhcv", ww, S)
        o = jnp.einsum("bhck,bhkv->bhcv", qg, S) + jnp.einsum("bhij,bhjv->bhiv", aqk, v_new)
        S = jnp.exp(gl)[..., None, None] * S + jnp.einsum("bhck,bhcv->bhkv", ke, v_new)
        return S, o

    s_fin, o = lax.scan(step, s0.astype(f32), (q_g, k_end, u, w, a_qk, g_last))
    return _from_chunks(o), s_fin


def _cross_attn(q, mem_k, mem_v):
    s = jnp.einsum("bthd,bmhd->bhtm", q, mem_k).astype(jnp.float32) * HEAD_DIM ** -0.5
    p = jax.nn.softmax(s, axis=-1)
    o = jnp.einsum("bhtm,bmhd->bthd", p.astype(mem_v.dtype), mem_v)
    return o.reshape(o.shape[:2] + (-1,))


def _layer(x, mem_k, mem_v, pos, lb, state, p):
    f32 = jnp.float32
    dt = x.dtype
    h_lru, conv_lru, conv_gdn, s_gdn, s_hgrn, s_ret = state
    (xa, gdn_qkv, gdn_z, gdn_b, gdn_a, hg_q, hg_f, hg_i, hg_g,
     rt_q, rt_k, rt_v, rt_g, xq) = _split(x @ p["w_in"], IN_SPLITS)

    xa_c, conv_lru_new = _causal_conv(xa, conv_lru, p["lru_conv_w"], p["lru_conv_b"])
    y_a, h_lru_new = _rglru(xa_c, h_lru, p["lru_wa"], p["lru_ba"], p["lru_wi"], p["lru_bi"], p["lru_lambda"])

    qkv_c, conv_gdn_new = _causal_conv(gdn_qkv, conv_gdn, p["gdn_conv_w"])
    gq, gk, gv = (_heads(t) for t in jnp.split(jax.nn.silu(qkv_c), 3, axis=-1))
    gq = _l2norm(gq) * HEAD_DIM ** -0.5
    gk = _l2norm(gk)
    beta = jax.nn.sigmoid(gdn_b.astype(f32))
    g_log = -jnp.exp(p["gdn_a_log"].astype(f32)) * jax.nn.softplus(gdn_a.astype(f32) + p["gdn_dt_bias"].astype(f32))
    o_b, s_gdn_new = _chunked_gdn(gq, gk, gv, g_log, beta, s_gdn, GDN_CHUNK)
    y_b = _head_rmsnorm(o_b, p["gdn_norm_w"]) * jax.nn.silu(gdn_z.astype(f32))

    fz = hg_f.astype(f32)
    log_f = jnp.logaddexp(jnp.log(lb), jnp.log1p(-lb) + jax.nn.log_sigmoid(fz))
    k_c = (1.0 - lb) * jax.nn.sigmoid(-fz)
    o_c, s_hgrn_new = _chunked_gla(_heads(jax.nn.silu(hg_q.astype(f32))), _heads(k_c), _heads(hg_i),
                                   _heads(log_f), s_hgrn, HGRN_CHUNK)
    y_c = _head_rmsnorm(o_c, p["hgrn_norm_w"]) * jax.nn.sigmoid(hg_g.astype(f32))

    rq = _rotary(_heads(rt_q), pos)
    rk = _rotary(_heads(rt_k), pos) * HEAD_DIM ** -0.5
    log_gamma = jnp.log1p(-jnp.exp2(-5.0 - jnp.arange(N_HEADS, dtype=f32)))
    log_f_d = jnp.broadcast_to(log_gamma[:, None], rq.shape)
    o_d, s_ret_new = _chunked_gla(rq, rk, _heads(rt_v), log_f_d, s_ret, RET_CHUNK)
    y_d = _head_groupnorm(o_d, p["ret_gn_w"], p["ret_gn_b"]) * jax.nn.silu(rt_g.astype(f32))

    y_e = _cross_attn(_heads(xq), mem_k, mem_v)

    merged = jnp.zeros(x.shape, f32)
    for n, y in enumerate((y_a, y_b, y_c, y_d, y_e)):
        gate = jax.nn.sigmoid((x @ p["w_merge_gate"][n] + p["b_merge_gate"][n]).astype(f32))
        merged = merged + gate * (y.astype(dt) @ p["w_branch"][n]).astype(f32)
    mix = merged.astype(dt) @ p["w_out"]
    x = _layernorm(DEEPNORM_ALPHA * x + mix, p["ln1_g"], p["ln1_b"])

    hg, hv = jnp.split(x @ p["w_ffn_up"], 2, axis=-1)
    ffn = (jax.nn.silu(hg) * hv) @ p["w_ffn_down"]
    x = _layernorm(DEEPNORM_ALPHA * x + ffn, p["ln2_g"], p["ln2_b"])
    return x, (h_lru_new, conv_lru_new, conv_gdn_new, s_gdn_new, s_hgrn_new, s_ret_new)


def setup_inputs(seed: int = 0) -> dict:
    key = jax.random.key(seed)
    ks = iter(jax.random.split(key, 48))
    f32 = jnp.float32

    def nrm(shape, scale):
        return jax.random.normal(next(ks), shape, f32) * scale

    def unif(shape, lo, hi):
        return jax.random.uniform(next(ks), shape, f32, lo, hi)

    L, D, W, H, HD = DEPTH, D_MODEL, W_BR, N_HEADS, HEAD_DIM
    log_s = jnp.log(unif((L, W), 0.9, 0.999)) / LRU_C
    lru_lambda = log_s - jnp.log(-jnp.expm1(log_s))
    dt = jnp.exp(unif((L, H), math.log(1e-3), math.log(1e-1)))
    gdn_dt_bias = dt + jnp.log(-jnp.expm1(-dt))
    gdn_a_log = jnp.log(unif((L, H), 1.0, 16.0))
    return {
        "x_prompt": nrm((BATCH, SEQ, D), 1.0),
        "x_sample": nrm((DEC_BATCH, DEC_SEQ, D), 1.0),
        "mem_prompt": nrm((BATCH, N_MEM, D), 1.0),
        "state_lru_h": nrm((L, DEC_BATCH, W), 0.5),
        "state_lru_conv": nrm((L, DEC_BATCH, CONV_W - 1, W), 1.0),
        "state_gdn_conv": nrm((L, DEC_BATCH, CONV_W - 1, 3 * W), 1.0),
        "state_gdn_s": nrm((L, DEC_BATCH, H, HD, HD), 0.1),
        "state_hgrn_s": nrm((L, DEC_BATCH, H, HD, HD), 0.3),
        "state_ret_s": nrm((L, DEC_BATCH, H, HD, HD), 1.0),
        "cache_mem_k": nrm((L, DEC_BATCH, N_MEM, H, HD), 1.0),
        "cache_mem_v": nrm((L, DEC_BATCH, N_MEM, H, HD), 1.0),
        "w_in": nrm((L, D, N_IN), D ** -0.5),
        "lru_conv_w": nrm((L, CONV_W, W), CONV_W ** -0.5),
        "lru_conv_b": nrm((L, W), 0.01),
        "lru_wa": nrm((L, H, HD, HD), HD ** -0.5),
        "lru_ba": nrm((L, W), 0.01),
        "lru_wi": nrm((L, H, HD, HD), HD ** -0.5),
        "lru_bi": nrm((L, W), 0.01),
        "lru_lambda": lru_lambda,
        "gdn_conv_w": nrm((L, CONV_W, 3 * W), CONV_W ** -0.5),
        "gdn_a_log": gdn_a_log,
        "gdn_dt_bias": gdn_dt_bias,
        "gdn_norm_w": 1.0 + nrm((L, W), 0.02),
        "hgrn_lb_raw": nrm((L, W), 0.1),
        "hgrn_norm_w": 1.0 + nrm((L, W), 0.02),
        "ret_gn_w": 1.0 + nrm((L, W), 0.02),
        "ret_gn_b": nrm((L, W), 0.01),
        "w_mem_k": nrm((L, D, W), D ** -0.5),
        "w_mem_v": nrm((L, D, W), D ** -0.5),
        "w_merge_gate": nrm((L, N_BR, D, D), D ** -0.5),
        "b_merge_gate": nrm((L, N_BR, D), 0.01),
        "w_branch": nrm((L, N_BR, W, D), W ** -0.5 * DEEPNORM_BETA),
        "w_out": nrm((L, D, D), D ** -0.5 * DEEPNORM_BETA),
        "ln1_g": 1.0 + nrm((L, D), 0.02),
        "ln1_b": nrm((L, D), 0.01),
        "w_ffn_up": nrm((L, D, 2 * D_FF), D ** -0.5),
        "w_ffn_down": nrm((L, D_FF, D), D_FF ** -0.5 * DEEPNORM_BETA),
        "ln2_g": 1.0 + nrm((L, D), 0.02),
        "ln2_b": nrm((L, D), 0.01),
    }


def reference(x_prompt, x_sample, mem_prompt,
              state_lru_h, state_lru_conv, state_gdn_conv, state_gdn_s, state_hgrn_s, state_ret_s,
              cache_mem_k, cache_mem_v,
              w_in, lru_conv_w, lru_conv_b, lru_wa, lru_ba, lru_wi, lru_bi, lru_lambda,
              gdn_conv_w, gdn_a_log, gdn_dt_bias, gdn_norm_w,
              hgrn_lb_raw, hgrn_norm_w, ret_gn_w, ret_gn_b,
              w_mem_k, w_mem_v, w_merge_gate, b_merge_gate, w_branch, w_out,
              ln1_g, ln1_b, w_ffn_up, w_ffn_down, ln2_g, ln2_b):
    f32 = jnp.float32
    B_p, T_p = x_prompt.shape[:2]
    T_s = x_sample.shape[1]
    n_mem = mem_prompt.shape[1]
    pos_p = jnp.arange(T_p)
    pos_s = PAST_LEN + jnp.arange(T_s)
    lb_cum = jnp.cumsum(jax.nn.softmax(hgrn_lb_raw.astype(f32), axis=0), axis=0)
    lb_all = lb_cum - lb_cum[0]
    zero_state = (jnp.zeros((B_p, W_BR), f32),
                  jnp.zeros((B_p, CONV_W - 1, W_BR), x_prompt.dtype),
                  jnp.zeros((B_p, CONV_W - 1, 3 * W_BR), x_prompt.dtype),
                  jnp.zeros((B_p, N_HEADS, HEAD_DIM, HEAD_DIM), f32),
                  jnp.zeros((B_p, N_HEADS, HEAD_DIM, HEAD_DIM), f32),
                  jnp.zeros((B_p, N_HEADS, HEAD_DIM, HEAD_DIM), f32))
    xp, xs = x_prompt, x_sample
    new_p, new_s, mem_k_p, mem_v_p = [], [], [], []
    for l in range(DEPTH):
        p = dict(w_in=w_in[l], lru_conv_w=lru_conv_w[l], lru_conv_b=lru_conv_b[l], lru_wa=lru_wa[l],
                 lru_ba=lru_ba[l], lru_wi=lru_wi[l], lru_bi=lru_bi[l], lru_lambda=lru_lambda[l],
                 gdn_conv_w=gdn_conv_w[l], gdn_a_log=gdn_a_log[l], gdn_dt_bias=gdn_dt_bias[l],
                 gdn_norm_w=gdn_norm_w[l], hgrn_norm_w=hgrn_norm_w[l], ret_gn_w=ret_gn_w[l],
                 ret_gn_b=ret_gn_b[l], w_merge_gate=w_merge_gate[l], b_merge_gate=b_merge_gate[l],
                 w_branch=w_branch[l], w_out=w_out[l], ln1_g=ln1_g[l], ln1_b=ln1_b[l],
                 w_ffn_up=w_ffn_up[l], w_ffn_down=w_ffn_down[l], ln2_g=ln2_g[l], ln2_b=ln2_b[l])
        mk = (mem_prompt @ w_mem_k[l]).reshape(B_p, n_mem, N_HEADS, HEAD_DIM)
        mv = (mem_prompt @ w_mem_v[l]).reshape(B_p, n_mem, N_HEADS, HEAD_DIM)
        xp, sp = _layer(xp, mk, mv, pos_p, lb_all[l], zero_state, p)
        st_s = (state_lru_h[l], state_lru_conv[l], state_gdn_conv[l], state_gdn_s[l], state_hgrn_s[l], state_ret_s[l])
        xs, ss = _layer(xs, cache_mem_k[l], cache_mem_v[l], pos_s, lb_all[l], st_s, p)
        new_p.append(sp)
        new_s.append(ss)
        mem_k_p.append(mk)
        mem_v_p.append(mv)
    return (xp, xs,
            jnp.stack([s[0] for s in new_p]), jnp.stack([s[1] for s in new_p]), jnp.stack([s[2] for s in new_p]),
            jnp.stack([s[3] for s in new_p]), jnp.stack([s[4] for s in new_p]), jnp.stack([s[5] for s in new_p]),
            jnp.stack(mem_k_p), jnp.stack(mem_v_p),
            jnp.stack([s[0] for s in new_s]), jnp.stack([s[1] for s in new_s]), jnp.stack([s[2] for s in new_s]),
            jnp.stack([s[3] for s in new_s]), jnp.stack([s[4] for s in new_s]), jnp.stack([s[5] for s in new_s]))
```

```python
import contextlib
import numpy as np
import concourse.bass as bass
import concourse.mybir as mybir
from concourse.bass_utils import run_bass_kernel_spmd

F32 = mybir.dt.float32
BF16 = mybir.dt.bfloat16
AF = mybir.ActivationFunctionType
ALU = mybir.AluOpType
AX = mybir.AxisListType

L = 2
D = 1024
T = 2048
TB = 512
NBLK = T // TB
NS = 16
NCORE = 8
W = 512
H = 4
HD = 128
DFF = 2816
NIN = 7176
NMEM = 256
PAST = 16384
ALPHA = (2 * L) ** 0.25
SCALE = HD ** -0.5
LN_EPS = 1e-5
NEPS = 1e-6
HC = 64
BIG = 30000.0
GAM = [1.0 - 2.0 ** (-5.0 - h) for h in range(H)]

SEG = dict(xa=(0, 512), gqkv=(512, 1536), gz=(2048, 512), gba=(2560, 8), hq=(2568, 512),
           hf=(3080, 512), hi=(3592, 512), hg=(4104, 512), rq=(4616, 512), rk=(5128, 512),
           rv=(5640, 512), rg=(6152, 512), xq=(6664, 512))

PROWS = [("lru_conv_w", 16), ("lru_conv_b", 4), ("lru_ba", 4), ("lru_bi", 4), ("lru_lambda", 4),
         ("gdn_conv_w", 48), ("gdn_norm_w", 4), ("lb0", 4), ("lb1", 4), ("hgrn_norm_w", 4),
         ("ret_gn_w", 4), ("ret_gn_b", 4), ("ln1_g", 8), ("ln1_b", 8), ("ln2_g", 8),
         ("ln2_b", 8), ("b_merge_gate", 40)]
PCOL = {}
_r = 0
for _n, _k in PROWS:
    PCOL[_n] = _r
    _r += _k
NPROW = _r


STAGELOG = []
import os as _osg
STRICT_SYNC = bool(int(_osg.environ.get('STRICT_SYNC', '1')))


class Sched:
    ENG = ("pe", "act", "dve", "pool", "sp")

    def __init__(self, nc, n_dma_sems=40):
        self.nc = nc
        self.e = {"pe": nc.tensor, "act": nc.scalar, "dve": nc.vector, "pool": nc.gpsimd, "sp": nc.sync}
        self.sem = {k: nc.alloc_semaphore(name="s_" + k) for k in self.ENG}
        self.cnt = {k: 0 for k in self.ENG}
        self.known = {k: {} for k in self.ENG}
        self.snaps = {k: {} for k in self.ENG}
        self.dsem = [nc.alloc_semaphore(name="d%d" % i) for i in range(n_dma_sems)]
        self.dcnt = [0] * n_dma_sems
        self.dpool = {"pool": list(range(0, 8)), "sp": list(range(8, 26)), "act": list(range(26, n_dma_sems))}
        self.dnext = {"pool": 0, "sp": 0, "act": 0}
        self.res = {}
        self.n_wait = 0
        self.n_ins = 0

    def _semh(self, sk):
        return self.dsem[sk[1]] if isinstance(sk, tuple) else self.sem[sk]

    def _wait(self, eng, tok, hazard="raw"):
        sk, v = tok
        if self.known[eng].get(sk, 0) >= v:
            return
        if sk == eng:
            if eng in ("pe", "sp"):
                return
            if not STRICT_SYNC and (hazard == "war" or v < self.cnt[eng]):
                return
        self.e[eng].wait_ge(self._semh(sk), v)
        self.n_wait += 1
        kn = self.known[eng]
        kn[sk] = v
        if not isinstance(sk, tuple):
            sn = self.snaps[sk].get(v)
            if sn:
                for k2, v2 in sn.items():
                    if kn.get(k2, 0) < v2:
                        kn[k2] = v2

    def _deps(self, eng, reads, writes):
        for r in reads:
            st = self.res.get(r)
            if st and st[0]:
                self._wait(eng, st[0], "raw")
        for r in writes:
            st = self.res.get(r)
            if st:
                if st[0]:
                    self._wait(eng, st[0], "waw")
                for sk, v in st[1].items():
                    self._wait(eng, (sk, v), "war")

    def _record(self, tok, reads, writes):
        for r in reads:
            st = self.res.setdefault(r, [None, {}])
            if st[1].get(tok[0], 0) < tok[1]:
                st[1][tok[0]] = tok[1]
        for r in writes:
            self.res[r] = [tok, {}]

    def op(self, eng, fn, reads=(), writes=()):
        self._deps(eng, reads, writes)
        ins = fn(self.e[eng])
        self.cnt[eng] += 1
        ins.then_inc(self.sem[eng], 1)
        tok = (eng, self.cnt[eng])
        self.snaps[eng][self.cnt[eng]] = dict(self.known[eng])
        self._record(tok, reads, writes)
        self.n_ins += 1
        return tok

    def dma(self, q, out, in_, reads=(), writes=(), **kw):
        pl = self.dpool[q]
        i = pl[self.dnext[q]]
        self.dnext[q] = (self.dnext[q] + 1) % len(pl)
        sk = ("d", i)
        if self.dcnt[i] > 0:
            self._wait(q, (sk, 16 * self.dcnt[i]))
        self._deps(q, reads, writes)
        ins = self.e[q].dma_start(out=out, in_=in_, **kw)
        self.dcnt[i] += 1
        ins.then_inc(self.dsem[i], 16)
        tok = (sk, 16 * self.dcnt[i])
        self._record(tok, reads, writes)
        self.n_ins += 1
        return tok

    def barrier(self):
        for eng in ("act", "dve", "sp"):
            for i, c in enumerate(self.dcnt):
                if c and i not in self.dpool["pool"]:
                    self._wait(eng, (("d", i), 16 * c))
            for k in ("pe", "act", "dve", "sp"):
                if k != eng and self.cnt[k]:
                    self._wait(eng, (k, self.cnt[k]))

    def finish(self, eng="sp"):
        for i, c in enumerate(self.dcnt):
            if c:
                self._wait(eng, (("d", i), 16 * c))
        for k in self.ENG:
            if k != eng and self.cnt[k]:
                self._wait(eng, (k, self.cnt[k]))


def _rn(ap):
    try:
        return ap.tensor.name
    except AttributeError:
        return ap.name


def _keys(ap):
    try:
        t = ap.tensor
    except AttributeError:
        return [ap.name]
    name = t.name
    shp = tuple(t.shape)
    if len(shp) < 3 or "DRam" in type(t).__name__:
        return [name]
    A = shp[1]
    G = 1
    for d_ in shp[2:]:
        G *= d_
    F = A * G
    pairs = list(ap.ap)
    lo = ap.offset % F
    ext = 1
    for st_, c_ in pairs[1:]:
        ext += (c_ - 1) * abs(st_)
    b0 = lo // G
    b1 = min(A - 1, (lo + ext - 1) // G)
    return [(name, b) for b in range(b0, b1 + 1)]


class KB:
    def __init__(self, nc):
        self.nc = nc
        self.S = Sched(nc)
        self.uid = 0
        self.stack = []
        self.caches = [{}]
        self.rr = 0

    def sb(self, name, shape, dt=F32):
        key = (name, tuple(shape), str(dt))
        cache = self.caches[-1]
        if key in cache:
            return cache[key]
        self.uid += 1
        t = self.stack[-1].enter_context(self.nc.sbuf_tensor("%s_%d" % (name, self.uid), list(shape), dt))
        cache[key] = t
        return t

    @contextlib.contextmanager
    def scope(self):
        es = contextlib.ExitStack()
        self.stack.append(es)
        self.caches.append({})
        try:
            yield
        finally:
            self.S.barrier()
            self.stack.pop()
            self.caches.pop()
            es.close()

    def bank(self):
        b = self.ps[self.rr]
        self.rr = (self.rr + 1) % self.n_rot
        return b

    def _rw(self, ins, outs):
        r = []
        for a in ins:
            if not isinstance(a, (int, float)) and a is not None:
                r += _keys(a)
        w = []
        for a in outs:
            w += _keys(a)
        w = w + [n for n in r if isinstance(n, str) and n.startswith("ps") and n not in w]
        return r, w

    def mm(self, out, lhsT, rhs, start=True, stop=True):
        r, w = self._rw([lhsT, rhs], [out])
        self.S.op("pe", lambda e: e.matmul(out, lhsT, rhs, start=start, stop=stop), r, w)

    def tr(self, out, in_, ident):
        r, w = self._rw([in_, ident], [out])
        self.S.op("pe", lambda e: e.transpose(out, in_, ident), r, w)

    def act(self, out, in_, func, bias=0.0, scale=1.0, accum=None, eng="act"):
        r, w = self._rw([in_, bias, scale], [out] + ([accum] if accum is not None else []))
        if accum is None:
            self.S.op("act", lambda e: e.activation(out, in_, func, bias=bias, scale=scale), r, w)
        else:
            self.S.op("act", lambda e: e.activation(out, in_, func, bias=bias, scale=scale, accum_out=accum), r, w)

    def cp(self, out, in_, eng="act"):
        r, w = self._rw([in_], [out])
        if eng == "act":
            self.S.op("act", lambda e: e.copy(out, in_), r, w)
        else:
            self.S.op(eng, lambda e: e.tensor_copy(out, in_), r, w)

    def tt(self, out, a, b, op, eng="dve"):
        r, w = self._rw([a, b], [out])
        self.S.op(eng, lambda e: e.tensor_tensor(out, a, b, op), r, w)

    def ts(self, out, a, s1, op0, s2=None, op1=None, eng="dve"):
        r, w = self._rw([a, s1, s2], [out])
        if op1 is None:
            self.S.op(eng, lambda e: e.tensor_scalar(out, a, s1, None, op0), r, w)
        else:
            self.S.op(eng, lambda e: e.tensor_scalar(out, a, s1, s2, op0, op1), r, w)

    def stt(self, out, in0, scalar, in1, op0, op1):
        r, w = self._rw([in0, scalar, in1], [out])
        self.S.op("dve", lambda e: e.scalar_tensor_tensor(out, in0, scalar, in1, op0, op1), r, w)

    def scan(self, out, d0, d1, init, op0, op1):
        r, w = self._rw([d0, d1, init], [out])
        self.S.op("dve", lambda e: e.tensor_tensor_scan(out, d0, d1, init, op0, op1), r, w)

    def recip(self, out, in_):
        r, w = self._rw([in_], [out])
        self.S.op("dve", lambda e: e.reciprocal(out, in_), r, w)

    def rsum(self, out, in_):
        r, w = self._rw([in_], [out])
        self.S.op("dve", lambda e: e.reduce_sum(out, in_, AX.X), r, w)

    def memset(self, ap, v, eng="dve"):
        r, w = self._rw([], [ap])
        self.S.op(eng, lambda e: e.memset(ap, v), r, w)

    def dma(self, out, in_, q="sp", **kw):
        r, w = self._rw([in_], [out])
        self.S.dma(q, out, in_, r, w, **kw)


def build_program():
    nc = bass.Bass("TRN2", target_bir_lowering=False)
    K = KB(nc)

    def din(name, shape):
        return nc.dram_tensor(name, list(shape), F32, kind="ExternalInput").ap()

    def dout(name, shape):
        return nc.dram_tensor(name, list(shape), F32, kind="ExternalOutput").ap()

    x_prompt = din("x_prompt", [T, D])
    x_sample = din("x_sample", [NS, D])
    mem_prompt = din("mem_prompt", [NMEM, D])
    st_lru_h = din("state_lru_h", [L, NS, W])
    st_lru_conv = din("state_lru_conv", [L, NS, 3, W])
    st_gdn_conv = din("state_gdn_conv", [L, NS, 3, 3 * W])
    st_gdn_s = din("state_gdn_s", [L, NS, H, HD, HD])
    st_hgrn_s = din("state_hgrn_s", [L, NS, H, HD, HD])
    st_ret_s = din("state_ret_s", [L, NS, H, HD, HD])
    cache_k = din("cache_mem_k", [L, NS, NMEM, W])
    cache_v = din("cache_mem_v", [L, NS, NMEM, W])
    w_in = din("w_in", [L, D, NIN])
    lru_conv_w = din("lru_conv_w", [L, 4, W])
    lru_conv_b = din("lru_conv_b", [L, W])
    lru_wa = din("lru_wa", [L, H, HD, HD])
    lru_ba = din("lru_ba", [L, W])
    lru_wi = din("lru_wi", [L, H, HD, HD])
    lru_bi = din("lru_bi", [L, W])
    lru_lambda = din("lru_lambda", [L, W])
    gdn_conv_w = din("gdn_conv_w", [L, 4, 3 * W])
    gdn_a_log = din("gdn_a_log", [L, H])
    gdn_dt_bias = din("gdn_dt_bias", [L, H])
    gdn_norm_w = din("gdn_norm_w", [L, W])
    hgrn_lb_raw = din("hgrn_lb_raw", [L, W])
    hgrn_norm_w = din("hgrn_norm_w", [L, W])
    ret_gn_w = din("ret_gn_w", [L, W])
    ret_gn_b = din("ret_gn_b", [L, W])
    w_mem_k = din("w_mem_k", [L, D, W])
    w_mem_v = din("w_mem_v", [L, D, W])
    w_merge_gate = din("w_merge_gate", [L, 5, D, D])
    b_merge_gate = din("b_merge_gate", [L, 5, D])
    w_branch = din("w_branch", [L, 5, W, D])
    w_out = din("w_out", [L, D, D])
    ln1_g = din("ln1_g", [L, D])
    ln1_b = din("ln1_b", [L, D])
    w_ffn_up = din("w_ffn_up", [L, D, 2 * DFF])
    w_ffn_down = din("w_ffn_down", [L, DFF, D])
    ln2_g = din("ln2_g", [L, D])
    ln2_b = din("ln2_b", [L, D])
    c_ident = din("c_ident", [128, 128])
    c_uincl = din("c_uincl", [128, 128])
    c_ugt = din("c_ugt", [128, 128])
    c_pswap = din("c_pswap", [128, 128])
    c_posm = din("c_posm", [128, 128])
    c_negm = din("c_negm", [128, 128])
    c_dmask = din("c_dmask", [128, H * 128])
    c_decq = din("c_decq", [128, H * 128])
    c_dendr = din("c_dendr", [128, H * 128])
    c_cos = din("c_cos", [128, T])
    c_sin = din("c_sin", [128, T])
    c_cs_s = din("c_cs_s", [128, 2])
    c_reset = din("c_reset", [128, TB])

    y_prompt = dout("y_prompt", [T, D])
    y_sample = dout("y_sample", [NS, D])
    o_p_lru_h = dout("p_lru_h", [L, W])
    o_p_lru_conv = dout("p_lru_conv", [L, 3, W])
    o_p_gdn_conv = dout("p_gdn_conv", [L, 3, 3 * W])
    o_p_gdn_s = dout("p_gdn_s", [L, H, HD, HD])
    o_p_hgrn_s = dout("p_hgrn_s", [L, H, HD, HD])
    o_p_ret_s = dout("p_ret_s", [L, H, HD, HD])
    o_p_mem_k = dout("p_mem_k", [L, NMEM, W])
    o_p_mem_v = dout("p_mem_v", [L, NMEM, W])
    o_s_lru_h = dout("s_lru_h", [L, NS, W])
    o_s_lru_conv = dout("s_lru_conv", [L, NS, 3, W])
    o_s_gdn_conv = dout("s_gdn_conv", [L, NS, 3, 3 * W])
    o_s_gdn_s = dout("s_gdn_s", [L, NS, H, HD, HD])
    o_s_hgrn_s = dout("s_hgrn_s", [L, NS, H, HD, HD])
    o_s_ret_s = dout("s_ret_s", [L, NS, H, HD, HD])

    root = contextlib.ExitStack()
    K.stack.append(root)
    K.ps = [root.enter_context(nc.psum_tensor("ps%d" % i, [128, 512], F32)) for i in range(8)]
    K.n_rot = 4
    PD = K.ps[4:8]

    ident = K.sb("ident", [128, 128])
    ones = K.sb("ones", [128, 128])
    onesb = K.sb("onesb", [128, 128], BF16)
    uincl = K.sb("uincl", [128, 128])
    ugt = K.sb("ugt", [128, 128])
    pswap = K.sb("pswap", [128, 128])
    posm = K.sb("posm", [128, 128])
    negm = K.sb("negm", [128, 128])
    dmask = K.sb("dmask", [128, H, 128])
    decq = K.sb("decq", [128, H, 128])
    dendr = K.sb("dendr", [128, H, 128])
    cs_s = K.sb("cs_s", [128, 2])
    resetm = K.sb("resetm", [128, TB])
    for t_, d_ in ((ident, c_ident), (uincl, c_uincl), (ugt, c_ugt), (pswap, c_pswap), (posm, c_posm),
                   (negm, c_negm), (cs_s, c_cs_s), (resetm, c_reset)):
        K.dma(t_[:], d_)
    for t_, d_ in ((dmask, c_dmask), (decq, c_decq), (dendr, c_dendr)):
        K.dma(t_[:], d_.rearrange("p (h c) -> p h c", h=H))
    K.memset(ones[:], 1.0)
    K.memset(onesb[:], 1.0)
    sel8 = K.sb("sel8", [8, 8, 128])
    for r_ in range(8):
        K.ts(sel8[:, r_, :], ones[0:8, :], ident[0:8, r_:r_ + 1], ALU.mult)

    NWB = 4
    wbuf = [K.sb("wb%d" % i, [128, 4096], BF16) for i in range(NWB)]
    wstate = {"i": 0}

    NWT = 50
    wscr = [nc.dram_tensor("wscr%d" % l, [NWT, 128, 4096], BF16).ap() for l in range(L)]
    cur = {"blk": 0, "l": 0, "widx": 0, "pend": None}

    def wflush():
        if cur["pend"] is not None:
            dst, srcv = cur["pend"]
            K.dma(dst, srcv, q="pool")
            cur["pend"] = None

    def wload(src2d, nk, ncols, cache=True):
        b = wbuf[wstate["i"]]
        wstate["i"] = (wstate["i"] + 1) % NWB
        flat = b[:, 0:nk * ncols]
        v = flat.rearrange("p (k c) -> p k c", k=nk)
        if not cache:
            K.dma(v, src2d.rearrange("(k p) c -> p k c", p=128), q="pool")
            return v
        idx = cur["widx"]
        cur["widx"] += 1
        assert idx < NWT
        scr = wscr[cur["l"]][idx][:, 0:nk * ncols]
        if cur["blk"] == 0:
            K.dma(v, src2d.rearrange("(k p) c -> p k c", p=128), q="pool")
            wflush()
            cur["pend"] = (scr, flat)
        else:
            K.dma(flat, scr, q="pool")
        return v

    Sst = {}
    for l in range(L):
        for br in ("gdn", "hgrn", "ret"):
            Sst[(l, br)] = K.sb("S_%s%d" % (br, l), [128, H, 128])
            K.memset(Sst[(l, br)][:], 0.0)
    hcar = [K.sb("hcar%d" % l, [128, 4]) for l in range(L)]
    halo_a = [K.sb("haloa%d" % l, [128, 4, 3]) for l in range(L)]
    halo_g = [K.sb("halog%d" % l, [128, 12, 3]) for l in range(L)]
    for l in range(L):
        K.memset(hcar[l][:], 0.0)
        K.memset(halo_a[l][:], 0.0)
        K.memset(halo_g[l][:], 0.0)
    memKT = [K.sb("memKT%d" % l, [128, H, NMEM], BF16) for l in range(L)]
    memV = [K.sb("memV%d" % l, [128, 2, W], BF16) for l in range(L)]
    xmT = K.sb("xmT", [128, 8, NMEM], BF16)
    pc = [K.sb("pc%d" % l, [128, NPROW + 8]) for l in range(L)]
    wa_bf = [K.sb("wa%d" % l, [128, H, 128], BF16) for l in range(L)]
    wi_bf = [K.sb("wi%d" % l, [128, H, 128], BF16) for l in range(L)]
    gbc = [K.sb("gbc%d" % l, [128, 8]) for l in range(L)]
    der = [K.sb("der%d" % l, [128, 16]) for l in range(L)]

    class Grp:
        pass

    P = Grp()
    P.N = TB
    P.prompt = True
    P.xf = K.sb("p_xf", [128, 8, TB])
    P.xb = K.sb("p_xb", [128, 8, TB], BF16)
    Sg = Grp()
    Sg.N = NS
    Sg.prompt = False
    Sg.xf = K.sb("s_xf", [128, 8, NS])
    Sg.xb = K.sb("s_xb", [128, 8, NS], BF16)

    import os as _os0
    with K.scope():
        ptm = K.sb("ptm", [128, 2, 128])
        K.memset(ptm[:], 0.0)
        for l in range(L if not (int(_os0.environ.get("DBGX", "0")) & 4) else 0):
            def prow(name, src2d):
                r0 = PCOL[name]
                n = src2d.shape[0]
                K.dma(ptm[r0 % 128:r0 % 128 + n, r0 // 128, :], src2d)
            prow("lru_conv_w", lru_conv_w[l].rearrange("t (c p) -> (t c) p", p=128))
            prow("lru_conv_b", lru_conv_b[l].rearrange("(c p) -> c p", p=128))
            prow("lru_ba", lru_ba[l].rearrange("(c p) -> c p", p=128))
            prow("lru_bi", lru_bi[l].rearrange("(c p) -> c p", p=128))
            prow("lru_lambda", lru_lambda[l].rearrange("(c p) -> c p", p=128))
            prow("gdn_conv_w", gdn_conv_w[l].rearrange("t (c p) -> (t c) p", p=128))
            prow("gdn_norm_w", gdn_norm_w[l].rearrange("(c p) -> c p", p=128))
            prow("lb0", hgrn_lb_raw[0].rearrange("(c p) -> c p", p=128))
            prow("lb1", hgrn_lb_raw[1].rearrange("(c p) -> c p", p=128))
            prow("hgrn_norm_w", hgrn_norm_w[l].rearrange("(c p) -> c p", p=128))
            prow("ret_gn_w", ret_gn_w[l].rearrange("(c p) -> c p", p=128))
            prow("ret_gn_b", ret_gn_b[l].rearrange("(c p) -> c p", p=128))
            prow("ln1_g", ln1_g[l].rearrange("(c p) -> c p", p=128))
            prow("ln1_b", ln1_b[l].rearrange("(c p) -> c p", p=128))
            prow("ln2_g", ln2_g[l].rearrange("(c p) -> c p", p=128))
            prow("ln2_b", ln2_b[l].rearrange("(c p) -> c p", p=128))
            prow("b_merge_gate", b_merge_gate[l].rearrange("n (c p) -> (n c) p", p=128))
            pb = K.bank()
            K.tr(pb[:, 0:128], ptm[:, 0, :], ident[:])
            K.tr(pb[:, 128:128 + 48], ptm[0:48, 1, :], ident[0:48, 0:48])
            K.cp(pc[l][:, 0:NPROW], pb[:, 0:NPROW])
            lam = pc[l][:, PCOL["lru_lambda"]:PCOL["lru_lambda"] + 4]
            tmp = K.sb("tmpd", [128, 4])
            K.act(tmp[:], lam, AF.Exp, scale=-1.0)
            K.act(tmp[:], tmp[:], AF.Ln, bias=1.0)
            K.ts(der[l][:, 0:4], tmp[:], -8.0, ALU.mult)
            K.ts(der[l][:, 4:8], tmp[:], -16.0, ALU.mult)
            if l == 0:
                K.memset(der[l][:, 8:12], 0.0)
                K.memset(der[l][:, 12:16], 1.0)
            else:
                d_ = K.sb("tmpd2", [128, 4])
                K.tt(d_[:], pc[l][:, PCOL["lb1"]:PCOL["lb1"] + 4], pc[l][:, PCOL["lb0"]:PCOL["lb0"] + 4], ALU.subtract)
                K.act(der[l][:, 8:12], d_[:], AF.Sigmoid)
                K.ts(der[l][:, 12:16], der[l][:, 8:12], -1.0, ALU.mult, 1.0, ALU.add)
            K.dma(wa_bf[l][:], lru_wa[l].rearrange("h i j -> i h j"), q="pool")
            K.dma(wi_bf[l][:], lru_wi[l].rearrange("h i j -> i h j"), q="pool")
            K.dma(gbc[l][:, 0:4], gdn_a_log[l:l + 1, :].partition_broadcast(128))
            K.dma(gbc[l][:, 4:8], gdn_dt_bias[l:l + 1, :].partition_broadcast(128))
            K.act(gbc[l][:, 0:4], gbc[l][:, 0:4], AF.Exp)
            K.ts(gbc[l][:, 0:4], gbc[l][:, 0:4], -1.0, ALU.mult)

    def pcol(l, name, i=0):
        c = PCOL[name] + i
        return pc[l][:, c:c + 1]

    def project(l, seg, groups, handler):
        c0, ncol_tot = SEG[seg]
        K.n_rot = 8
        try:
            _project(l, seg, groups, handler)
        finally:
            K.n_rot = 4
            K.rr = 0

    def _project(l, seg, groups, handler):
        c0, ncol_tot = SEG[seg]
        for cc in range(0, ncol_tot, 512):
            ncol = min(512, ncol_tot - cc)
            wt = wload(w_in[l, :, c0 + cc:c0 + cc + ncol], 8, ncol)
            for g in groups:
                for j in range(ncol // 128):
                    p = K.bank()
                    for k in range(8):
                        K.mm(p[:, :g.N], wt[:, k, j * 128:(j + 1) * 128], g.xb[:, k, :g.N], start=(k == 0), stop=(k == 7))
                    handler(g, cc // 128 + j, p[:, :g.N])

    def ln_stat_step(g, zf, c, pm_ap, pq_ap):
        zsq = K.sb("zsq", [128, 8, g.N])
        K.act(zsq[:, c, :], zf[:, c, :], AF.Square)
        K.mm(pm_ap, ones[:], zf[:, c, :], start=(c == 0), stop=(c == 7))
        K.mm(pq_ap, ones[:], zsq[:, c, :], start=(c == 0), stop=(c == 7))

    def layernorm(l, g, zf, gname, bname, stats=None):
        N = g.N
        zsq = K.sb("zsq", [128, 8, N])
        if stats is None:
            pm = K.bank()[:, :N]
            pq = K.bank()[:, :N]
            for c in range(8):
                ln_stat_step(g, zf, c, pm, pq)
        else:
            pm, pq = stats
        mu = K.sb("mu", [128, N])
        K.act(mu[:], pm, AF.Copy, scale=1.0 / D)
        musq = K.sb("musq", [128, N])
        K.act(musq[:], mu[:], AF.Square)
        var = K.sb("var", [128, N])
        K.stt(var[:], pq, 1.0 / D, musq[:], ALU.mult, ALU.subtract)
        K.act(var[:], var[:], AF.Ln, bias=LN_EPS)
        rstd = K.sb("rstd", [128, N])
        K.act(rstd[:], var[:], AF.Exp, scale=-0.5)
        for c in range(8):
            t = zsq[:, c, :]
            en = "dve"
            K.tt(t, zf[:, c, :], mu[:], ALU.subtract, eng=en)
            K.tt(t, t, rstd[:], ALU.mult, eng=en)
            K.ts(g.xf[:, c, :N], t, pcol(l, gname, c), ALU.mult, pcol(l, bname, c), ALU.add, eng=en)
            K.cp(g.xb[:, c, :N], g.xf[:, c, :N])

    def head_rms(l, g, po, wname, h, gate, yout):
        N = g.N
        osq = K.sb("osq%d" % (h % 2), [128, N])
        K.act(osq[:], po, AF.Square)
        pq = K.bank()
        K.mm(pq[:, :N], ones[:], osq[:])
        sd = K.sb("sd%d" % (h % 2), [128, N])
        K.act(sd[:], pq[:, :N], AF.Ln, bias=NEPS, scale=1.0 / HD)
        K.act(sd[:], sd[:], AF.Exp, scale=-0.5)
        K.stt(osq[:], po, pcol(l, wname, h), sd[:], ALU.mult, ALU.mult)
        K.tt(yout, osq[:], gate, ALU.mult)

    def head_gn(l, g, po, h, gate, yout):
        N = g.N
        osb = K.sb("osb%d" % (h % 2), [128, N])
        K.cp(osb[:], po)
        osq = K.sb("osq2%d" % (h % 2), [128, N])
        K.act(osq[:], po, AF.Square)
        pm = K.bank()
        pq = K.bank()
        K.mm(pm[:, :N], ones[:], osb[:])
        K.mm(pq[:, :N], ones[:], osq[:])
        mu = K.sb("gmu%d" % (h % 2), [128, N])
        K.act(mu[:], pm[:, :N], AF.Copy, scale=1.0 / HD)
        K.act(osq[:], mu[:], AF.Square)
        var = K.sb("gvar%d" % (h % 2), [128, N])
        K.stt(var[:], pq[:, :N], 1.0 / HD, osq[:], ALU.mult, ALU.subtract)
        K.act(var[:], var[:], AF.Ln, bias=NEPS)
        K.act(var[:], var[:], AF.Exp, scale=-0.5)
        K.tt(osb[:], osb[:], mu[:], ALU.subtract)
        K.tt(osb[:], osb[:], var[:], ALU.mult)
        K.ts(osb[:], osb[:], pcol(l, "ret_gn_w", h), ALU.mult, pcol(l, "ret_gn_b", h), ALU.add)
        K.tt(yout, osb[:], gate, ALU.mult)

    def branch_lru(l, blk, groups):
        with K.scope():
            for g in groups:
                N = g.N
                if g.prompt:
                    g.xa = K.sb("xa_h", [128, 4, 3 + N])
                    K.cp(g.xa[:, :, 0:3], halo_a[l][:], eng="dve")
                else:
                    g.xa = K.sb("xa_s", [128, 4, N, 4])
                    g.h0 = K.sb("h0_s", [128, 4, N])
                    stc = K.sb("stc", [48, W])
                    K.dma(stc[:], st_lru_conv[l].rearrange("b r c -> (b r) c"))
                    sth = K.sb("sth", [NS, W])
                    K.dma(sth[:], st_lru_h[l])
                    for c in range(4):
                        pb = K.bank()
                        K.tr(pb[:, 0:48], stc[:, c * 128:(c + 1) * 128], ident[0:48, 0:48])
                        K.tr(pb[:, 64:64 + NS], sth[:, c * 128:(c + 1) * 128], ident[0:NS, 0:NS])
                        K.cp(g.xa[:, c, :, 0:3], pb[:, 0:48].rearrange("p (b r) -> p b r", r=3))
                        K.cp(g.h0[:, c, :], pb[:, 64:64 + NS])

            def ev(g, j, p):
                N = g.N
                if g.prompt:
                    K.cp(g.xa[:, j, 3:3 + N], p)
                    tp_ = lambda t: g.xa[:, j, t:t + N]
                else:
                    K.cp(g.xa[:, j, :, 3], p)
                    tp_ = lambda t: g.xa[:, j, :, t]
                xc = K.sb("xc%d" % j, [128, N])
                K.ts(xc[:], tp_(3), pcol(l, "lru_conv_w", 12 + j), ALU.mult, pcol(l, "lru_conv_b", j), ALU.add)
                for t in (2, 1, 0):
                    K.stt(xc[:], tp_(t), pcol(l, "lru_conv_w", 4 * t + j), xc[:], ALU.mult, ALU.add)
                xcb = K.sb("xcb%d" % j, [128, N], BF16)
                K.cp(xcb[:], xc[:], eng="dve")
            project(l, "xa", groups, ev)
            for g in groups:
                N = g.N
                if g.prompt:
                    K.cp(halo_a[l][:], g.xa[:, :, N:N + 3], eng="dve")
                    tap = lambda j, t: g.xa[:, j, t:t + N]
                else:
                    tap = lambda j, t: g.xa[:, j, :, t]
                lru_rg = []
                for j in range(4):
                    xc = K.sb("xc%d" % j, [128, N])
                    xcb = K.sb("xcb%d" % j, [128, N], BF16)
                    p1 = K.bank()
                    p2 = K.bank()
                    K.mm(p1[:, :N], wa_bf[l][:, j, :], xcb[:])
                    K.mm(p2[:, :N], wi_bf[l][:, j, :], xcb[:])
                    r_ = K.sb("lr%d" % j, [128, N])
                    ig = K.sb("lig%d" % j, [128, N])
                    K.act(r_[:], p1[:, :N], AF.Sigmoid, bias=pcol(l, "lru_ba", j))
                    K.act(ig[:], p2[:, :N], AF.Sigmoid, bias=pcol(l, "lru_bi", j))
                    lru_rg.append((xc, r_, ig))
                for j in range(4):
                    xc, r_, ig = lru_rg[j]
                    a_ = K.sb("la%d" % (j % 2), [128, N])
                    a2 = K.sb("la2%d" % (j % 2), [128, N])
                    K.act(a_[:], r_[:], AF.Exp, scale=der[l][:, j:j + 1])
                    K.act(a2[:], r_[:], AF.Exp, scale=der[l][:, 4 + j:5 + j])
                    K.act(a2[:], a2[:], AF.Ln, bias=1.0, scale=-1.0)
                    K.act(a2[:], a2[:], AF.Exp, scale=0.5)
                    K.tt(ig[:], ig[:], xc[:], ALU.mult)
                    K.tt(ig[:], ig[:], a2[:], ALU.mult)
                    hh = K.sb("lh%d" % (j % 2), [128, N])
                    if g.prompt:
                        K.scan(hh[:], a_[:], ig[:], hcar[l][:, j:j + 1], ALU.mult, ALU.add)
                        K.cp(hcar[l][:, j:j + 1], hh[:, N - 1:N], eng="dve")
                    else:
                        K.tt(hh[:], a_[:], g.h0[:, j, :], ALU.mult)
                        K.tt(hh[:], hh[:], ig[:], ALU.add)
                        K.cp(g.h0[:, j, :], hh[:], eng="dve")
                    K.cp(g.y[0][:, j, :], hh[:])
                if g.prompt:
                    if blk == NBLK - 1:
                        K.dma(o_p_lru_h[l].rearrange("(c p) -> p c", p=128), hcar[l][:], q="act", allow_slow_non_contiguous=True)
                        for j in range(4):
                            K.dma(o_p_lru_conv[l][:, j * 128:(j + 1) * 128].rearrange("r p -> p r"), halo_a[l][:, j, :], q="act",
                                  allow_slow_non_contiguous=True)
                else:
                    ost = K.sb("ost", [NS, 2, W])
                    pb = K.bank()
                    pb2 = K.bank()
                    for c in range(4):
                        K.tr(pb[0:NS, c * 128:(c + 1) * 128], g.h0[:, c, :], ident[:])
                        K.tr(pb2[0:NS, c * 128:(c + 1) * 128], g.xa[:, c, :, 3], ident[:])
                    K.cp(ost[:, 0, :], pb[0:NS, :])
                    K.cp(ost[:, 1, :], pb2[0:NS, :])
                    K.dma(o_s_lru_h[l], ost[:, 0, :], q="act")
                    K.dma(o_s_lru_conv[l][:, 2, :], ost[:, 1, :], q="act")
                    K.dma(o_s_lru_conv[l][:, 0:2, :], st_lru_conv[l][:, 1:3, :], q="act")


    def branch_gdn(l, blk, groups):
        with K.scope():
            for g in groups:
                N = g.N
                g.gz = K.sb("gz", [128, 4, N], BF16)
                if g.prompt:
                    g.gq = K.sb("gq_h", [128, 12, 3 + N])
                    K.cp(g.gq[:, :, 0:3], halo_g[l][:], eng="dve")
                    g.gba = K.sb("gba_tm", [128, 4, 8])
                else:
                    g.gq = K.sb("gq_s", [128, 12, N, 4])
                    g.gs = K.sb("gqs_s", [128, 12, N])
                    g.gba = K.sb("gba_fm", [8, N])
                    for part in range(3):
                        stc = K.sb("stcg", [48, W])
                        K.dma(stc[:], st_gdn_conv[l][:, :, part * W:(part + 1) * W].rearrange("b r c -> (b r) c"))
                        for c in range(4):
                            pb = K.bank()
                            K.tr(pb[:, 0:48], stc[:, c * 128:(c + 1) * 128], ident[0:48, 0:48])
                            K.cp(g.gq[:, part * 4 + c, :, 0:3], pb[:, 0:48].rearrange("p (b r) -> p b r", r=3))

            def ev_qkv(g, j, p):
                N = g.N
                if g.prompt:
                    K.cp(g.gq[:, j, 3:3 + N], p)
                    K.cp(halo_g[l][:, j, :], g.gq[:, j, N:N + 3], eng="dve")
                    tap = lambda t: g.gq[:, j, t:t + N]
                else:
                    K.cp(g.gq[:, j, :, 3], p)
                    tap = lambda t: g.gq[:, j, :, t]
                xc = K.sb("gxc%d" % (j % 2), [128, N])
                K.ts(xc[:], tap(3), pcol(l, "gdn_conv_w", 36 + j), ALU.mult)
                for t in (2, 1, 0):
                    K.stt(xc[:], tap(t), pcol(l, "gdn_conv_w", 12 * t + j), xc[:], ALU.mult, ALU.add)
                if g.prompt:
                    if blk == NBLK - 1:
                        K.dma(o_p_gdn_conv[l][:, j * 128:(j + 1) * 128].rearrange("r p -> p r"), halo_g[l][:, j, :], q="act",
                              allow_slow_non_contiguous=True)
                    K.act(g.gq[:, j, 3:3 + N], xc[:], AF.Silu)
                else:
                    K.act(g.gs[:, j, :], xc[:], AF.Silu)
            project(l, "gqkv", groups, ev_qkv)

            def ev_z(g, j, p):
                K.act(g.gz[:, j, :], p, AF.Silu)
            project(l, "gz", groups, ev_z)
            wba = wload(w_in[l, :, 2560:2568], 8, 8)
            for g in groups:
                if g.prompt:
                    for tt in range(4):
                        p = K.bank()
                        for k in range(8):
                            K.mm(p[:, 0:8], g.xb[:, k, tt * 128:(tt + 1) * 128], wba[:, k, :], start=(k == 0), stop=(k == 7))
                        K.cp(g.gba[:, tt, :], p[:, 0:8])
                else:
                    p = K.bank()
                    for k in range(8):
                        K.mm(p[0:8, :g.N], wba[:, k, :], g.xb[:, k, :g.N], start=(k == 0), stop=(k == 7))
                    K.cp(g.gba[:], p[0:8, :g.N])
            for g in groups:
                with K.scope():
                    if g.prompt:
                        gdn_prompt(l, blk, g)
                    else:
                        gdn_sample_pre(l, g)
                        gdn_sample_step(l, g)

    def gdn_prompt(l, blk, g):
        N = g.N
        S_ = Sst[(l, "gdn")]
        HS = range(H)
        betaA = K.sb("gbetaA", [128, 16])
        nbetaA = K.sb("gnbetaA", [128, 16])
        ggA = K.sb("gggA", [128, 16])
        GcA = K.sb("gGcA", [128, 2, 16])
        eGA = K.sb("geGA", [128, 2, 16])
        nGA = K.sb("gnGA", [128, 16])
        K.act(betaA[:].rearrange("p (t h) -> p t h", h=4), g.gba[:, :, 0:4], AF.Exp, scale=-1.0)
        K.ts(betaA[:], betaA[:], 1.0, ALU.add)
        K.recip(betaA[:], betaA[:])
        K.ts(nbetaA[:], betaA[:], -1.0, ALU.mult)
        for tt in range(4):
            K.tt(ggA[:, tt * 4:(tt + 1) * 4], g.gba[:, tt, 4:8], gbc[l][:, 4:8], ALU.add)
        K.act(ggA[:], ggA[:], AF.Exp)
        K.act(ggA[:], ggA[:], AF.Ln, bias=1.0)
        for tt in range(4):
            K.tt(ggA[:, tt * 4:(tt + 1) * 4], ggA[:, tt * 4:(tt + 1) * 4], gbc[l][:, 0:4], ALU.mult)
        pG = K.bank()
        K.mm(pG[:, 0:16], uincl[:], ggA[:])
        K.mm(pG[:, 16:32], ugt[:], ggA[:])
        K.cp(GcA[:].rearrange("p a c -> p (a c)"), pG[:, 0:32])
        K.act(eGA[:], GcA[:], AF.Exp)
        K.ts(nGA[:], GcA[:, 0, :], -1.0, ALU.mult)
        ssqA = K.sb("gssqA", [128, 32])
        rsA = K.sb("grsA", [128, 32])
        pS = K.bank()
        for j in range(8):
            sq = K.sb("gsqb", [128, N], BF16)
            K.act(sq[:], g.gq[:, j, 3:3 + N], AF.Square)
            for tt in range(4):
                K.mm(pS[:, tt * 8 + j:tt * 8 + j + 1], sq[:, tt * 128:(tt + 1) * 128], onesb[:, 0:1])
        K.act(ssqA[:], pS[:, 0:32], AF.Ln, bias=NEPS)
        K.act(rsA[:], ssqA[:], AF.Exp, scale=-0.5)
        for tt in range(4):
            c0 = 3 + tt * 128
            ts_ = slice(tt * 4, (tt + 1) * 4)
            beta = betaA[:, ts_]
            nbeta = nbetaA[:, ts_]
            Gc = GcA[:, 0, ts_]
            eG = eGA[:, 0, ts_]
            eE = eGA[:, 1, ts_]
            nG = nGA[:, ts_]
            rs = rsA[:, tt * 8:(tt + 1) * 8]
            qkv = [K.sb("qkv%d" % h, [128, 384]) for h in HS]
            for h in HS:
                p = K.bank()
                for i_ in range(3):
                    K.tr(p[:, i_ * 128:(i_ + 1) * 128], g.gq[:, i_ * 4 + h, c0:c0 + 128], ident[:])
                K.cp(qkv[h][:], p[:, 0:384])
            tm = [K.sb("gtm%d" % h, [128, 4, 128]) for h in HS]
            Y = [K.sb("gY%d" % h, [128, 256]) for h in HS]
            for h in HS:
                q_, k_, v_ = qkv[h][:, 0:128], qkv[h][:, 128:256], qkv[h][:, 256:384]
                K.ts(tm[h][:, 0, :], k_, rs[:, 4 + h:5 + h], ALU.mult)
                K.ts(tm[h][:, 1, :], q_, rs[:, h:h + 1], ALU.mult, SCALE, ALU.mult)
                K.ts(tm[h][:, 2, :], tm[h][:, 1, :], eG[:, h:h + 1], ALU.mult)
                K.ts(tm[h][:, 3, :], tm[h][:, 0, :], eE[:, h:h + 1], ALU.mult)
                K.ts(Y[h][:, 0:128], v_, beta[:, h:h + 1], ALU.mult)
                K.ts(Y[h][:, 128:256], tm[h][:, 0, :], beta[:, h:h + 1], ALU.mult, eG[:, h:h + 1], ALU.mult)
            DEC = [K.sb("gDEC%d" % h, [128, 256]) for h in HS]
            dS = K.sb("gdS", [128, 4])
            for h in HS:
                dg = K.sb("gdg", [128, 128])
                K.ts(dg[:], ident[:], Gc[:, h:h + 1], ALU.mult)
                pR = K.bank()
                K.mm(pR[:, 0:128], ones[:], dg[:])
                t1 = K.sb("gt1", [128, 256])
                K.tt(t1[:, 0:128], pR[:, 0:128], posm[:], ALU.add)
                K.tt(t1[:, 128:256], pR[:, 0:128], negm[:], ALU.add)
                K.act(DEC[h][:, 0:128], t1[:, 0:128], AF.Exp, bias=Gc[:, h:h + 1], scale=-1.0)
                K.act(DEC[h][:, 128:256], t1[:, 128:256], AF.Exp, bias=nG[:, h:h + 1], scale=1.0)
                K.act(dS[:, h:h + 1], pR[:, 127:128], AF.Exp)
            fmT = [K.sb("gfm%d" % h, [128, 384]) for h in HS]
            for h in HS:
                p = K.bank()
                for i_ in range(3):
                    K.tr(p[:, i_ * 128:(i_ + 1) * 128], tm[h][:, i_, :], ident[:])
                K.cp(fmT[h][:], p[:, 0:384])
            X = [[K.sb("gX%d_0" % h, [128, 256]), DEC[h]] for h in HS]
            AQ = [K.sb("gAQ%d" % h, [128, 128]) for h in HS]
            for h in HS:
                p = K.bank()
                K.mm(p[:, 0:128], fmT[h][:, 0:128], fmT[h][:, 0:128])
                K.mm(p[:, 128:256], fmT[h][:, 0:128], fmT[h][:, 128:256])
                K.stt(X[h][0][:, 0:128], p[:, 0:128], nbeta[:, h:h + 1], DEC[h][:, 0:128], ALU.mult, ALU.mult)
                K.tt(AQ[h][:], p[:, 128:256], DEC[h][:, 128:256], ALU.mult)
            for h in HS:
                p = K.bank()
                K.tr(p[:, 0:128], X[h][0][:, 0:128], ident[:])
                K.cp(X[h][0][:, 128:256], p[:, 0:128])
            for s in range(7):
                cur = s % 2
                for h in HS:
                    p = K.bank()
                    K.mm(p[:, 0:256], X[h][cur][:, 128:256], Y[h][:])
                    K.tt(Y[h][:], Y[h][:], p[:, 0:256], ALU.add)
                if s < 6:
                    for h in HS:
                        p = K.bank()
                        K.mm(p[:, 0:128], X[h][cur][:, 128:256], X[h][cur][:, 0:128])
                        K.mm(p[:, 128:256], X[h][cur][:, 0:128], X[h][cur][:, 128:256])
                        K.cp(X[h][1 - cur][:], p[:, 0:256])
            wT = [qkv[h][:, 0:128] for h in HS]
            for h in HS:
                p = K.bank()
                K.tr(p[:, 0:128], Y[h][:, 128:256], ident[:])
                K.cp(wT[h][:], p[:, 0:128])
            vn = [qkv[h][:, 128:256] for h in HS]
            for h in HS:
                p = K.bank()
                K.mm(p[:, 0:128], wT[h][:], S_[:, h, :])
                K.tt(vn[h][:], Y[h][:, 0:128], p[:, 0:128], ALU.subtract)
            for h in HS:
                K.mm(PD[h][:, tt * 128:(tt + 1) * 128], S_[:, h, :], fmT[h][:, 256:384], start=True, stop=False)
                K.mm(PD[h][:, tt * 128:(tt + 1) * 128], vn[h][:], AQ[h][:], start=False, stop=True)
            for h in HS:
                p = K.bank()
                K.mm(p[:, 0:128], tm[h][:, 3, :], vn[h][:])
                K.stt(S_[:, h, :], S_[:, h, :], dS[:, h:h + 1], p[:, 0:128], ALU.mult, ALU.add)
        for h in HS:
            head_rms(l, g, PD[h][:, :N], "gdn_norm_w", h, g.gz[:, h, :], g.y[1][:, h, :])
        if blk == NBLK - 1:
            K.dma(o_p_gdn_s[l].rearrange("h k v -> k h v"), S_[:], q="act")

    def gdn_sample_pre(l, g):
        N = g.N
        sq = K.sb("gs_sq", [128, 8, N])
        for j in range(8):
            K.act(sq[:, j, :], g.gs[:, j, :], AF.Square)
        pq = K.bank()
        K.mm(pq[:, 0:8 * N], ones[:], sq[:].rearrange("p a n -> p (a n)"))
        rs = K.sb("gs_rs", [128, 8, N])
        K.act(rs[:].rearrange("p a n -> p (a n)"), pq[:, 0:8 * N], AF.Ln, bias=NEPS)
        K.act(rs[:], rs[:], AF.Exp, scale=-0.5)
        g.qn = K.sb("gs_qn", [128, 4, N])
        g.kn = K.sb("gs_kn", [128, 4, N])
        K.tt(g.qn[:], g.gs[:, 0:4, :], rs[:, 0:4, :], ALU.mult)
        K.ts(g.qn[:], g.qn[:], SCALE, ALU.mult)
        K.tt(g.kn[:], g.gs[:, 4:8, :], rs[:, 4:8, :], ALU.mult)
        pb = K.bank()
        for r_ in range(8):
            K.mm(pb[:, r_ * N:(r_ + 1) * N], sel8[:, r_, :], g.gba[:])
        bc = K.sb("gs_bc", [128, 8, N])
        K.cp(bc[:].rearrange("p a n -> p (a n)"), pb[:, 0:8 * N])
        g.beta = K.sb("gs_beta", [128, 4, N])
        g.eg = K.sb("gs_eg", [128, 4, N])
        K.act(g.beta[:], bc[:, 0:4, :], AF.Exp, scale=-1.0)
        K.ts(g.beta[:], g.beta[:], 1.0, ALU.add)
        K.recip(g.beta[:], g.beta[:])
        for h in range(H):
            K.ts(g.eg[:, h, :], bc[:, 4 + h, :], gbc[l][:, 4 + h:5 + h], ALU.add)
        K.act(g.eg[:], g.eg[:], AF.Exp)
        K.act(g.eg[:], g.eg[:], AF.Ln, bias=1.0)
        for h in range(H):
            K.ts(g.eg[:, h, :], g.eg[:, h, :], gbc[l][:, h:h + 1], ALU.mult)
        K.act(g.eg[:], g.eg[:], AF.Exp)
        g.kb = K.sb("gs_kb", [128, 4, N])
        g.nkn = K.sb("gs_nkn", [128, 4, N])
        K.tt(g.kb[:], g.kn[:], g.beta[:], ALU.mult)
        K.ts(g.nkn[:], g.kn[:], -1.0, ALU.mult)
        g.vtm_g = K.sb("gs_vtm", [NS, W])
        pv = K.bank()
        for c in range(4):
            K.tr(pv[0:NS, c * 128:(c + 1) * 128], g.gs[:, 8 + c, :], ident[:])
        K.cp(g.vtm_g[:], pv[0:NS, :])
        ost = K.sb("gs_ost", [NS, 3 * W])
        for part in range(3):
            pb2 = K.bank()
            for c in range(4):
                K.tr(pb2[0:NS, c * 128:(c + 1) * 128], g.gq[:, part * 4 + c, :, 3], ident[:])
            K.cp(ost[:, part * W:(part + 1) * W], pb2[0:NS, :])
        K.dma(o_s_gdn_conv[l][:, 2, :], ost[:], q="act")
        K.dma(o_s_gdn_conv[l][:, 0:2, :], st_gdn_conv[l][:, 1:3, :], q="act")


    def chunk_core(g, S_, C, qe_b, ke_b, qi, kend_f, v_f, maskfn, dsfn):
        N = g.N
        nch = N // C
        for n in range(nch):
            cs = slice(n * C, (n + 1) * C)
            vk = []
            atm = []
            for h in range(H):
                p = K.bank()
                K.mm(p[0:C, 0:C], ke_b[h][:, cs], qe_b[h][:, cs])
                K.tr(p[0:C, 128:256], v_f[h][:, cs], ident[:])
                K.tr(p[0:C, 256:384], kend_f[h][:, cs], ident[:])
                a = K.sb("cc_at%d_%d" % (h, n % 2), [C, C], BF16)
                K.tt(a[:], p[0:C, 0:C], maskfn(h), ALU.mult)
                v = K.sb("cc_vk%d_%d" % (h, n % 2), [C, 256], BF16)
                K.cp(v[:], p[0:C, 128:384])
                atm.append(a)
                vk.append(v)
            for h in range(H):
                K.mm(PD[h][:, cs], vk[h][:, 0:128], atm[h][:], start=True, stop=False)
                K.mm(PD[h][:, cs], S_[:, h, :], qi[h][:, cs], start=False, stop=True)
            for h in range(H):
                p = K.bank()
                K.mm(p[:, 0:128], vk[h][:, 128:256], vk[h][:, 0:128])
                K.stt(S_[:, h, :], S_[:, h, :], dsfn(h, n), p[:, 0:128], ALU.mult, ALU.add)

    def hgrn_prompt_pre(g):
        N = g.N
        C = HC
        nch = N // C
        qeb, keb, kendf, dS = [], [], [], []
        for h in range(H):
            G = K.sb("hG%d" % (h % 2), [128, N])
            K.act(g.lf[h][:], g.lf[h][:], AF.Ln)
            K.scan(G[:], resetm[:, :N], g.lf[h][:], 0.0, ALU.mult, ALU.add)
            e1 = K.sb("he1_%d" % (h % 2), [128, N])
            K.act(e1[:], G[:], AF.Exp)
            K.tt(g.hq[h][:], g.hq[h][:], e1[:], ALU.mult)
            qb = K.sb("hqb%d" % h, [128, N], BF16)
            K.cp(qb[:], g.hq[h][:])
            K.act(e1[:], G[:], AF.Exp, scale=-1.0)
            kb_ = K.sb("hkb%d" % h, [128, N], BF16)
            K.tt(kb_[:], g.kc[h][:], e1[:], ALU.mult)
            ds_ = K.sb("hdS%d" % h, [128, nch])
            Gv = G[:].rearrange("p (n c) -> p n c", c=C)
            K.act(ds_[:], Gv[:, :, C - 1], AF.Exp)
            K.tt(e1[:].rearrange("p (n c) -> p n c", c=C), Gv, Gv[:, :, C - 1:C].to_broadcast([128, nch, C]), ALU.subtract)
            K.act(e1[:], e1[:], AF.Exp, scale=-1.0)
            K.tt(g.kc[h][:], g.kc[h][:], e1[:], ALU.mult)
            qeb.append(qb)
            keb.append(kb_)
            kendf.append(g.kc[h])
            dS.append(ds_)
        return qeb, keb, kendf, dS

    def branch_hgrn(l, blk, groups):
        with K.scope():
            for g in groups:
                N = g.N
                g.hq = [K.sb("hq%d" % h, [128, N]) for h in range(H)]
                g.kc = [K.sb("hkc%d" % h, [128, N]) for h in range(H)]
                g.lf = [K.sb("hlf%d" % h, [128, N]) for h in range(H)]
                g.hv = [K.sb("hv%d" % h, [128, N]) for h in range(H)]
                g.hgate = K.sb("hgate", [128, 4, N], BF16)

            def ev_q(g, j, p):
                K.act(g.hq[j][:], p, AF.Silu)

            def ev_f(g, j, p):
                f = g.lf[j]
                K.act(f[:], p, AF.Sigmoid)
                K.ts(f[:], f[:], der[l][:, 12 + j:13 + j], ALU.mult, der[l][:, 8 + j:9 + j], ALU.add)
                K.ts(g.kc[j][:], f[:], -1.0, ALU.mult, 1.0, ALU.add)

            def ev_i(g, j, p):
                K.cp(g.hv[j][:], p)

            def ev_g(g, j, p):
                K.act(g.hgate[:, j, :], p, AF.Sigmoid)
            project(l, "hq", groups, ev_q)
            project(l, "hf", groups, ev_f)
            hpre = {}
            for g in groups:
                if g.prompt:
                    hpre[id(g)] = hgrn_prompt_pre(g)
            project(l, "hi", groups, ev_i)
            project(l, "hg", groups, ev_g)
            for g in groups:
                if not g.prompt:
                    g.vtm_h = K.sb("hs_vtm", [NS, W])
                    pv = K.bank()
                    for c in range(4):
                        K.tr(pv[0:NS, c * 128:(c + 1) * 128], g.hv[c][:], ident[:])
                    K.cp(g.vtm_h[:], pv[0:NS, :])
                    gla_sample_step(l, g, st_hgrn_s, o_s_hgrn_s, g.vtm_h,
                                    lambda h, b: g.lf[h][:, b:b + 1], lambda h, b: g.kc[h][:, b:b + 1],
                                    lambda h, b: g.hq[h][:, b:b + 1],
                                    lambda h, po: head_rms(l, g, po, "hgrn_norm_w", h, g.hgate[:, h, :], g.y[2][:, h, :]))
                    continue
                N = g.N
                C = HC
                S_ = Sst[(l, "hgrn")]
                qeb, keb, kendf, dS = hpre[id(g)]
                chunk_core(g, S_, C, qeb, keb, g.hq, kendf, g.hv,
                           lambda h: uincl[0:C, 0:C], lambda h, n: dS[h][:, n:n + 1])
                for h in range(H):
                    head_rms(l, g, PD[h][:, :N], "hgrn_norm_w", h, g.hgate[:, h, :], g.y[2][:, h, :])
                if blk == NBLK - 1:
                    K.dma(o_p_hgrn_s[l].rearrange("h k v -> k h v"), S_[:], q="act")

    def branch_ret(l, blk, groups):
        with K.scope():
            for g in groups:
                N = g.N
                g.rq = [K.sb("rq%d" % h, [128, N]) for h in range(H)]
                g.rk = [K.sb("rk%d" % h, [128, N]) for h in range(H)]
                g.rv = [K.sb("rv%d" % h, [128, N]) for h in range(H)]
                g.rgate = K.sb("rgate", [128, 4, N], BF16)
                if g.prompt:
                    g.cos = K.sb("rcos", [128, N])
                    g.sin = K.sb("rsin", [128, N])
                    K.dma(g.cos[:], c_cos[:, blk * TB:(blk + 1) * TB])
                    K.dma(g.sin[:], c_sin[:, blk * TB:(blk + 1) * TB])

            def ev_q(g, j, p):
                K.cp(g.rq[j][:], p)

            def ev_k(g, j, p):
                K.cp(g.rk[j][:], p)

            def ev_v(g, j, p):
                K.cp(g.rv[j][:], p)

            def ev_g(g, j, p):
                K.act(g.rgate[:, j, :], p, AF.Silu)
            def rotary(g, tl, sc_, j):
                N = g.N
                p = K.bank()
                K.mm(p[:, :N], pswap[:], tl[:])
                t1 = K.sb("rt1_%d" % (j % 2), [128, N])
                t2 = K.sb("rt2_%d" % (j % 2), [128, N])
                if g.prompt:
                    K.stt(t1[:], tl[:], sc_, g.cos[:], ALU.mult, ALU.mult)
                    K.stt(t2[:], p[:, :N], sc_, g.sin[:], ALU.mult, ALU.mult)
                else:
                    K.ts(t1[:], tl[:], cs_s[:, 0:1], ALU.mult, sc_, ALU.mult)
                    K.ts(t2[:], p[:, :N], cs_s[:, 1:2], ALU.mult, sc_, ALU.mult)
                K.tt(tl[:], t1[:], t2[:], ALU.add)

            def ev_q2(g, j, p):
                ev_q(g, j, p)
                rotary(g, g.rq[j], 1.0, j)

            def ev_k2(g, j, p):
                ev_k(g, j, p)
                rotary(g, g.rk[j], SCALE, j)
            project(l, "rq", groups, ev_q2)
            project(l, "rk", groups, ev_k2)
            project(l, "rv", groups, ev_v)
            project(l, "rg", groups, ev_g)
            for g in groups:
                N = g.N
                if not g.prompt:
                    g.vtm_r = K.sb("rs_vtm", [NS, W])
                    pv = K.bank()
                    for c in range(4):
                        K.tr(pv[0:NS, c * 128:(c + 1) * 128], g.rv[c][:], ident[:])
                    K.cp(g.vtm_r[:], pv[0:NS, :])
                    gla_sample_step(l, g, st_ret_s, o_s_ret_s, g.vtm_r,
                                    lambda h, b: GAM[h], lambda h, b: g.rk[h][:, b:b + 1],
                                    lambda h, b: g.rq[h][:, b:b + 1],
                                    lambda h, po: head_gn(l, g, po, h, g.rgate[:, h, :], g.y[3][:, h, :]))
                    continue
                S_ = Sst[(l, "ret")]
                rqb, rkb, qi, kendf = [], [], [], []
                for h in range(H):
                    qb = K.sb("rqb%d" % h, [128, N], BF16)
                    kb_ = K.sb("rkb%d" % h, [128, N], BF16)
                    K.cp(qb[:], g.rq[h][:])
                    K.cp(kb_[:], g.rk[h][:])
                    K.tt(g.rq[h][:].rearrange("p (n c) -> p n c", c=128), g.rq[h][:].rearrange("p (n c) -> p n c", c=128),
                         decq[:, h:h + 1, :].to_broadcast([128, N // 128, 128]), ALU.mult)
                    K.tt(g.rk[h][:].rearrange("p (n c) -> p n c", c=128), g.rk[h][:].rearrange("p (n c) -> p n c", c=128),
                         dendr[:, h:h + 1, :].to_broadcast([128, N // 128, 128]), ALU.mult)
                    rqb.append(qb)
                    rkb.append(kb_)
                chunk_core(g, S_, 128, rqb, rkb, g.rq, g.rk, g.rv,
                           lambda h: dmask[:, h, :], lambda h, n: GAM[h] ** 128)
                for h in range(H):
                    head_gn(l, g, PD[h][:, :N], h, g.rgate[:, h, :], g.y[3][:, h, :])
                if blk == NBLK - 1:
                    K.dma(o_p_ret_s[l].rearrange("h k v -> k h v"), S_[:], q="act")

    def branch_xatt(l, blk, groups):
        with K.scope():
            for g in groups:
                g.cq = K.sb("cq", [128, 4, g.N], BF16 if g.prompt else F32)

            def ev(g, j, p):
                K.cp(g.cq[:, j, :], p)
            project(l, "xq", groups, ev)
            for g in groups:
                N = g.N
                if not g.prompt:
                    g.qtm = K.sb("xs_qtm", [NS, W])
                    pv = K.bank()
                    for c in range(4):
                        K.tr(pv[0:NS, c * 128:(c + 1) * 128], g.cq[:, c, :], ident[:])
                    K.cp(g.qtm[:], pv[0:NS, :])
                    xatt_sample_step(l, g)
                    continue
                for h in range(H):
                    E_ = K.sb("xE%d" % (h % 2), [128, 2, N], BF16)
                    for mc in range(2):
                        p = K.bank()
                        K.mm(p[:, :N], memKT[l][:, h, mc * 128:(mc + 1) * 128], g.cq[:, h, :])
                        K.act(E_[:, mc, :], p[:, :N], AF.Exp, scale=SCALE)
                    po = K.bank()
                    pd = K.bank()
                    for mc in range(2):
                        K.mm(po[:, :N], memV[l][:, mc, h * 128:(h + 1) * 128], E_[:, mc, :], start=(mc == 0), stop=(mc == 1))
                    for mc in range(2):
                        K.mm(pd[:, :N], onesb[:], E_[:, mc, :], start=(mc == 0), stop=(mc == 1))
                    rd = K.sb("xrd", [128, N])
                    K.recip(rd[:], pd[:, :N])
                    K.tt(g.y[4][:, h, :], po[:, :N], rd[:], ALU.mult)


    def selb_for(b):
        t = K.sb("selb%d" % (b % 4), [NS, 128])
        K.ts(t[:], ones[0:NS, :], ident[0:NS, b:b + 1], ALU.mult)
        return t

    def gdn_sample_step(l, g):
        po = PD[0]
        for b in range(NS):
            sel = selb_for(b)
            S_in = K.sb("ss_in%d" % (b % 4), [128, H, 128])
            K.dma(S_in[:], st_gdn_s[l, b].rearrange("h k v -> k h v"))
            Sp = K.sb("ss_p%d" % (b % 2), [128, H, 128])
            Sn = K.sb("ss_n%d" % (b % 4), [128, H, 128])
            nk = K.sb("ss_nk%d" % (b % 2), [128, H, 128])
            for h in range(H):
                K.ts(Sp[:, h, :], S_in[:, h, :], g.eg[:, h, b:b + 1], ALU.mult)
                K.ts(nk[:, h, :], ones[:], g.nkn[:, h, b:b + 1], ALU.mult)
            ps_ = []
            for h in range(H):
                hs = slice(h * 128, (h + 1) * 128)
                p1 = K.bank()
                K.mm(p1[:, 0:128], sel[:], g.vtm_g[:, hs], start=True, stop=False)
                K.mm(p1[:, 0:128], nk[:, h, :], Sp[:, h, :], start=False, stop=True)
                ps_.append(p1)
            for h in range(H):
                K.stt(Sn[:, h, :], ps_[h][:, 0:128], g.kb[:, h, b:b + 1], Sp[:, h, :], ALU.mult, ALU.add)
            for h in range(H):
                K.mm(po[:, h * NS + b:h * NS + b + 1], Sn[:, h, :], g.qn[:, h, b:b + 1])
            K.dma(o_s_gdn_s[l, b].rearrange("h k v -> k h v"), Sn[:], q="act")
        for h in range(H):
            head_rms(l, g, po[:, h * NS:(h + 1) * NS], "gdn_norm_w", h, g.gz[:, h, :], g.y[1][:, h, :])

    def gla_sample_step(l, g, st_in, st_out, vtm, fcol, kcol, qcol, post):
        po = PD[0]
        for b in range(NS):
            sel = selb_for(b)
            S_in = K.sb("ss_in%d" % (b % 2), [128, H, 128])
            K.dma(S_in[:], st_in[l, b].rearrange("h k v -> k h v"))
            Sn = K.sb("ss_n%d" % (b % 2), [128, H, 128])
            pv = K.bank()
            K.mm(pv[:, 0:W], sel[:], vtm[:])
            for h in range(H):
                K.ts(Sn[:, h, :], S_in[:, h, :], fcol(h, b), ALU.mult)
            for h in range(H):
                hs = slice(h * 128, (h + 1) * 128)
                K.stt(Sn[:, h, :], pv[:, hs], kcol(h, b), Sn[:, h, :], ALU.mult, ALU.add)
            for h in range(H):
                K.mm(po[:, h * NS + b:h * NS + b + 1], Sn[:, h, :], qcol(h, b))
            K.dma(st_out[l, b].rearrange("h k v -> k h v"), Sn[:], q="act")
        for h in range(H):
            post(h, po[:, h * NS:(h + 1) * NS])

    def xatt_sample_step(l, g):
        po = PD[0]
        E_all = K.sb("xs_E", [128, 2, NS * H])
        sT = K.sb("xs_sT", [128, 2, NS * H])
        for b in range(NS):
            sel = selb_for(b)
            Kc = K.sb("xs_K%d" % (b % 2), [128, 2, W])
            Vc = K.sb("xs_V%d" % (b % 2), [128, 2, W])
            K.dma(Kc[:], cache_k[l, b].rearrange("(mc m) c -> m mc c", m=128))
            K.dma(Vc[:], cache_v[l, b].rearrange("(mc m) c -> m mc c", m=128))
            pq = K.bank()
            K.mm(pq[:, 0:W], sel[:], g.qtm[:])
            qbs = K.sb("xs_qb", [128, W])
            K.cp(qbs[:], pq[:, 0:W])
            for mc in range(2):
                prod = K.sb("xs_prod", [128, W])
                K.tt(prod[:], Kc[:, mc, :], qbs[:], ALU.mult)
                K.rsum(sT[:, mc, b * 4:(b + 1) * 4], prod[:].rearrange("p (h d) -> p h d", h=H))
            K.act(E_all[:, :, b * 4:(b + 1) * 4], sT[:, :, b * 4:(b + 1) * 4], AF.Exp, scale=SCALE)
            for h in range(H):
                for mc in range(2):
                    K.mm(po[:, b * 4 + h:b * 4 + h + 1], Vc[:, mc, h * 128:(h + 1) * 128], E_all[:, mc, b * 4 + h:b * 4 + h + 1],
                         start=(mc == 0), stop=(mc == 1))
        pd = K.bank()
        for mc in range(2):
            K.mm(pd[:, 0:NS * H], ones[:], E_all[:, mc, :], start=(mc == 0), stop=(mc == 1))
        rd = K.sb("xs_rd", [128, NS * H])
        K.recip(rd[:], pd[:, 0:NS * H])
        for h in range(H):
            K.tt(g.y[4][:, h, :], po[:, 0:NS * H].rearrange("p (b h) -> p h b", h=H)[:, h, :],
                 rd[:].rearrange("p (b h) -> p h b", h=H)[:, h, :], ALU.mult)

    def merge(l, groups):
        with K.scope():
            K.n_rot = 8
            for g in groups:
                g.mg = K.sb("mg", [128, 8, g.N])
            for n in range(5):
                wb = wload(w_branch[l, n], 4, D)
                for half in range(2):
                    wg = wload(w_merge_gate[l, n][:, half * 512:(half + 1) * 512], 8, 512)
                    for g in groups:
                        N = g.N
                        for jj in range(4):
                            j = half * 4 + jj
                            pg = K.bank()
                            pz = K.bank()
                            for k in range(8):
                                K.mm(pg[:, :N], wg[:, k, jj * 128:(jj + 1) * 128], g.xb[:, k, :N], start=(k == 0), stop=(k == 7))
                            for k in range(4):
                                K.mm(pz[:, :N], wb[:, k, j * 128:(j + 1) * 128], g.y[n][:, k, :], start=(k == 0), stop=(k == 3))
                            gt = K.sb("gt%d" % (jj % 2), [128, N])
                            K.act(gt[:], pg[:, :N], AF.Sigmoid, bias=pcol(l, "b_merge_gate", n * 8 + j))
                            if n == 0:
                                K.tt(g.mg[:, j, :], gt[:], pz[:, :N], ALU.mult)
                            else:
                                K.tt(gt[:], gt[:], pz[:, :N], ALU.mult)
                                K.tt(g.mg[:, j, :], g.mg[:, j, :], gt[:], ALU.add)
            for g in groups:
                g.mb = K.sb("mb", [128, 8, g.N], BF16)
                g.z = K.sb("z1", [128, 8, g.N])
                for c in range(8):
                    K.cp(g.mb[:, c, :], g.mg[:, c, :])
            K.n_rot = 4
            K.rr = 0
            for g in groups:
                g.lnst = (K.ps[6][:, :g.N], K.ps[7][:, :g.N]) if g.prompt else (K.ps[4][:, 0:NS], K.ps[5][:, 0:NS])
            for half in range(2):
                wo = wload(w_out[l][:, half * 512:(half + 1) * 512], 8, 512)
                for g in groups:
                    N = g.N
                    for jj in range(4):
                        j = half * 4 + jj
                        p = K.bank()
                        for k in range(8):
                            K.mm(p[:, :N], wo[:, k, jj * 128:(jj + 1) * 128], g.mb[:, k, :], start=(k == 0), stop=(k == 7))
                        K.stt(g.z[:, j, :], g.xf[:, j, :N], ALPHA, p[:, :N], ALU.mult, ALU.add)
                        ln_stat_step(g, g.z, j, g.lnst[0], g.lnst[1])
            for g in groups:
                layernorm(l, g, g.z, "ln1_g", "ln1_b", stats=g.lnst)
            K.n_rot = 4
            K.rr = 0

    def ffn(l, groups):
        with K.scope():
            for g in groups:
                g.fa = K.sb("ffa", [128, 22, g.N], BF16)
                g.z = K.sb("z2", [128, 8, g.N])
            K.n_rot = 8
            for grp in range(6):
                j0 = grp * 4
                nj = min(4, 22 - j0)
                wgt = wload(w_ffn_up[l][:, j0 * 128:(j0 + nj) * 128], 8, nj * 128)
                wvt = wload(w_ffn_up[l][:, DFF + j0 * 128:DFF + (j0 + nj) * 128], 8, nj * 128)
                for g in groups:
                    N = g.N
                    for jj in range(nj):
                        pg = K.bank()
                        pv = K.bank()
                        for k in range(8):
                            K.mm(pg[:, :N], wgt[:, k, jj * 128:(jj + 1) * 128], g.xb[:, k, :N], start=(k == 0), stop=(k == 7))
                        for k in range(8):
                            K.mm(pv[:, :N], wvt[:, k, jj * 128:(jj + 1) * 128], g.xb[:, k, :N], start=(k == 0), stop=(k == 7))
                        sg = K.sb("fsg%d" % (jj % 2), [128, N])
                        K.act(sg[:], pg[:, :N], AF.Silu)
                        K.tt(g.fa[:, j0 + jj, :], sg[:], pv[:, :N], ALU.mult)
            K.n_rot = 4
            K.rr = 0
            for half in range(2):
                for kg in range(3):
                    nk = 8 if kg < 2 else 6
                    wd = wload(w_ffn_down[l][kg * 1024:kg * 1024 + nk * 128, half * 512:(half + 1) * 512], nk, 512)
                    for g in groups:
                        N = g.N
                        for jj in range(4):
                            acc = PD[jj][:, :N] if g.prompt else K.ps[jj][:, :N]
                            for kk in range(nk):
                                K.mm(acc, wd[:, kk, jj * 128:(jj + 1) * 128], g.fa[:, kg * 8 + kk, :],
                                     start=(kg == 0 and kk == 0), stop=(kg == 2 and kk == nk - 1))
                inter = len(groups) == 1
                for g in groups:
                    N = g.N
                    g.lnst = (K.ps[1][:, :N], K.ps[2][:, :N]) if inter else None
                    for jj in range(4):
                        acc = PD[jj][:, :N] if g.prompt else K.ps[jj][:, :N]
                        K.stt(g.z[:, half * 4 + jj, :], g.xf[:, half * 4 + jj, :N], ALPHA, acc, ALU.mult, ALU.add)
                        if inter:
                            ln_stat_step(g, g.z, half * 4 + jj, g.lnst[0], g.lnst[1])
            for g in groups:
                layernorm(l, g, g.z, "ln2_g", "ln2_b", stats=g.lnst)

    def mem_kv(l):
        with K.scope():
            if l == 0:
                for mt in range(2):
                    xin = K.sb("xmin", [128, D])
                    K.dma(xin[:], mem_prompt[mt * 128:(mt + 1) * 128, :])
                    for hf in range(2):
                        p = K.bank()
                        for c in range(4):
                            K.tr(p[:, c * 128:(c + 1) * 128], xin[:, (hf * 4 + c) * 128:(hf * 4 + c + 1) * 128], ident[:])
                        K.cp(xmT[:, hf * 4:hf * 4 + 4, mt * 128:(mt + 1) * 128], p[:].rearrange("p (c t) -> p c t", c=4))
            wk = wload(w_mem_k[l], 8, W, cache=False)
            for h in range(H):
                p = K.bank()
                for k in range(8):
                    K.mm(p[:, 0:NMEM], wk[:, k, h * 128:(h + 1) * 128], xmT[:, k, :], start=(k == 0), stop=(k == 7))
                K.cp(memKT[l][:, h, :], p[:, 0:NMEM])
            for mc in range(2):
                p = K.bank()
                for k in range(8):
                    K.mm(p[:, 0:W], xmT[:, k, mc * 128:(mc + 1) * 128], wk[:, k, :], start=(k == 0), stop=(k == 7))
                st = K.sb("mkst%d" % mc, [128, W])
                K.cp(st[:], p[:, 0:W])
                K.dma(o_p_mem_k[l][mc * 128:(mc + 1) * 128, :], st[:], q="act")
            wv = wload(w_mem_v[l], 8, W, cache=False)
            for mc in range(2):
                p = K.bank()
                for k in range(8):
                    K.mm(p[:, 0:W], xmT[:, k, mc * 128:(mc + 1) * 128], wv[:, k, :], start=(k == 0), stop=(k == 7))
                st = K.sb("mvst%d" % mc, [128, W])
                K.cp(st[:], p[:, 0:W])
                K.cp(memV[l][:, mc, :], p[:, 0:W], eng="dve")
                K.dma(o_p_mem_v[l][mc * 128:(mc + 1) * 128, :], st[:], q="act")

    import os as _os
    DBGX = int(_os.environ.get("DBGX", "0"))

    def load_x(blk):
        with K.scope():
            for tt in range(int(_os.environ.get("DBGT", "4"))):
                xin = K.sb("xin%d" % (tt % 2), [128, D])
                r0 = blk * TB + tt * 128
                K.dma(xin[:], x_prompt[r0:r0 + 128, :])
                for hf in range(2):
                    p = K.bank()
                    for c in range(4):
                        K.tr(p[:, c * 128:(c + 1) * 128], xin[:, (hf * 4 + c) * 128:(hf * 4 + c + 1) * 128], ident[:])
                    K.cp(P.xf[:, hf * 4:hf * 4 + 4, tt * 128:(tt + 1) * 128], p[:].rearrange("p (c t) -> p c t", c=4))
                    if not (DBGX & 2):
                        if DBGX & 8:
                            o_ = P.xb[:, hf * 4:hf * 4 + 4, tt * 128:(tt + 1) * 128]
                            i_ = p[:].rearrange("p (c t) -> p c t", c=4)
                            K.S.op("dve", lambda e: e.tensor_copy(o_, i_), _keys(p[:]) + _keys(P.xf[:]), _keys(P.xb[:]) + _keys(p[:]))
                        else:
                            K.cp(P.xb[:, hf * 4:hf * 4 + 4, tt * 128:(tt + 1) * 128], p[:].rearrange("p (c t) -> p c t", c=4), eng="dve")
            if blk == 0 and not (DBGX & 1):
                xs = K.sb("xins", [NS, D])
                K.dma(xs[:], x_sample)
                p = K.bank()
                for c in range(8):
                    K.tr(p[:, c * NS:(c + 1) * NS], xs[:, c * 128:(c + 1) * 128], ident[0:NS, 0:NS])
                K.cp(Sg.xf[:], p[:, 0:8 * NS].rearrange("p (c t) -> p c t", c=8))
                K.cp(Sg.xb[:], p[:, 0:8 * NS].rearrange("p (c t) -> p c t", c=8), eng="dve")

    def store_y(blk):
        with K.scope():
            for tt in range(4):
                yst = K.sb("yst%d" % (tt % 2), [128, D])
                for hf in range(2):
                    p = K.bank()
                    for c in range(4):
                        K.tr(p[:, c * 128:(c + 1) * 128], P.xf[:, hf * 4 + c, tt * 128:(tt + 1) * 128], ident[:])
                    K.cp(yst[:, hf * 512:(hf + 1) * 512], p[:])
                r0 = blk * TB + tt * 128
                K.dma(y_prompt[r0:r0 + 128, :], yst[:], q="act")
            if blk == 0:
                ys = K.sb("ysts", [NS, D])
                for hf in range(2):
                    p = K.bank()
                    for c in range(4):
                        K.tr(p[0:NS, c * 128:(c + 1) * 128], Sg.xf[:, hf * 4 + c, :], ident[:])
                    K.cp(ys[:, hf * 512:(hf + 1) * 512], p[0:NS, :])
                K.dma(y_sample, ys[:], q="act")

    import os
    kstop = int(os.environ.get("KSTOP", "1000000"))
    st_ = {"n": 0}

    class _Stop(Exception):
        pass

    def stage(tag):
        st_["n"] += 1
        STAGELOG.append((st_["n"], tag, dict(K.S.cnt)))
        if st_["n"] >= kstop:
            print("STOP at stage", st_["n"], tag)
            raise _Stop()

    try:
        stage("prep")
        for blk in range(NBLK):
            load_x(blk)
            stage("load_x")
            for l in range(L):
                groups = [P] + ([Sg] if blk == 0 else [])
                if blk == 0:
                    mem_kv(l)
                    stage("mem_kv")
                cur["blk"], cur["l"], cur["widx"] = blk, l, 0
                with K.scope():
                    for g in groups:
                        g.y = [K.sb("y%d" % n, [128, 4, g.N], BF16) for n in range(5)]
                    branch_lru(l, blk, groups)
                    stage("lru")
                    branch_gdn(l, blk, groups)
                    stage("gdn")
                    branch_hgrn(l, blk, groups)
                    stage("hgrn")
                    branch_ret(l, blk, groups)
                    stage("ret")
                    branch_xatt(l, blk, groups)
                    stage("xatt")
                    merge(l, groups)
                    stage("merge")
                ffn(l, groups)
                wflush()
                stage("ffn")
            store_y(blk)
            stage("store")
    except _Stop:
        pass
    K.S.finish("sp")
    K.stack.pop()
    root.close()
    return nc, K


_CONST_CACHE = {}
STAGELOG = []


def _consts():
    if _CONST_CACHE:
        return _CONST_CACHE
    f = np.float32
    idx = np.arange(128)
    c = {}
    c["c_ident"] = np.eye(128, dtype=f)
    c["c_uincl"] = (idx[:, None] <= idx[None, :]).astype(f)
    c["c_ugt"] = (idx[:, None] > idx[None, :]).astype(f)
    c["c_pswap"] = (idx[:, None] == (idx[None, :] + 64) % 128).astype(f)
    c["c_posm"] = np.where(idx[None, :] < idx[:, None], 0.0, BIG).astype(f)
    c["c_negm"] = np.where(idx[:, None] <= idx[None, :], 0.0, -BIG).astype(f)
    dm = np.zeros((128, H, 128), f)
    dq = np.zeros((128, H, 128), f)
    de = np.zeros((128, H, 128), f)
    for h in range(H):
        gam = np.float64(GAM[h])
        diff = idx[None, :] - idx[:, None]
        dm[:, h, :] = np.where(diff >= 0, gam ** np.maximum(diff, 0), 0.0)
        dq[:, h, :] = (gam ** (idx + 1))[None, :]
        de[:, h, :] = (gam ** (127 - idx))[None, :]
    c["c_dmask"] = dm.reshape(128, H * 128)
    c["c_decq"] = dq.reshape(128, H * 128)
    c["c_dendr"] = de.reshape(128, H * 128)
    half = 64
    inv = np.power(f(10000.0), -(np.arange(half, dtype=f) / f(half))).astype(f)
    pos = np.arange(T, dtype=f)
    ang = (pos[:, None] * inv[None, :]).astype(f)
    cos = np.cos(ang).astype(f).T
    sin = np.sin(ang).astype(f).T
    c["c_cos"] = np.ascontiguousarray(np.concatenate([cos, cos], 0))
    c["c_sin"] = np.ascontiguousarray(np.concatenate([-sin, sin], 0))
    angs = (f(PAST) * inv).astype(f)
    cs = np.zeros((128, 2), f)
    cs[:, 0] = np.concatenate([np.cos(angs), np.cos(angs)])
    cs[:, 1] = np.concatenate([-np.sin(angs), np.sin(angs)])
    c["c_cs_s"] = cs
    rm = np.ones((128, TB), f)
    rm[:, ::HC] = 0.0
    c["c_reset"] = rm
    _CONST_CACHE.update(c)
    return _CONST_CACHE


_PROG = {}


def kernel(**inp):
    f = np.float32
    if "nc" not in _PROG:
        _PROG["nc"], _ = build_program()
    nc = _PROG["nc"]
    consts = _consts()
    shared = {k: np.ascontiguousarray(inp[k], dtype=f) for k in (
        "w_in", "lru_conv_w", "lru_conv_b", "lru_wa", "lru_ba", "lru_wi", "lru_bi", "lru_lambda", "gdn_conv_w",
        "gdn_a_log", "gdn_dt_bias", "gdn_norm_w", "hgrn_lb_raw", "hgrn_norm_w", "ret_gn_w", "ret_gn_b",
        "w_mem_k", "w_mem_v", "w_merge_gate", "b_merge_gate", "w_branch", "w_out", "ln1_g", "ln1_b",
        "w_ffn_up", "w_ffn_down", "ln2_g", "ln2_b")}
    in_maps = []
    for c in range(NCORE):
        sl = slice(c * NS, (c + 1) * NS)
        m = dict(shared)
        m.update(consts)
        m["x_prompt"] = np.ascontiguousarray(inp["x_prompt"][c], dtype=f)
        m["x_sample"] = np.ascontiguousarray(inp["x_sample"][sl, 0, :], dtype=f)
        m["mem_prompt"] = np.ascontiguousarray(inp["mem_prompt"][c], dtype=f)
        m["state_lru_h"] = np.ascontiguousarray(inp["state_lru_h"][:, sl], dtype=f)
        m["state_lru_conv"] = np.ascontiguousarray(inp["state_lru_conv"][:, sl], dtype=f)
        m["state_gdn_conv"] = np.ascontiguousarray(inp["state_gdn_conv"][:, sl], dtype=f)
        m["state_gdn_s"] = np.ascontiguousarray(inp["state_gdn_s"][:, sl], dtype=f)
        m["state_hgrn_s"] = np.ascontiguousarray(inp["state_hgrn_s"][:, sl], dtype=f)
        m["state_ret_s"] = np.ascontiguousarray(inp["state_ret_s"][:, sl], dtype=f)
        m["cache_mem_k"] = np.ascontiguousarray(inp["cache_mem_k"][:, sl].reshape(L, NS, NMEM, W), dtype=f)
        m["cache_mem_v"] = np.ascontiguousarray(inp["cache_mem_v"][:, sl].reshape(L, NS, NMEM, W), dtype=f)
        in_maps.append(m)
    res = run_bass_kernel_spmd(nc, in_maps, core_ids=list(range(NCORE)))
    R = res.results

    def cat1(name, axis):
        return np.concatenate([np.asarray(R[c][name], dtype=f) for c in range(NCORE)], axis=axis)

    def stack1(name):
        return np.stack([np.asarray(R[c][name], dtype=f) for c in range(NCORE)], axis=1)
    y_p = np.stack([np.asarray(R[c]["y_prompt"], dtype=f) for c in range(NCORE)], axis=0)
    y_s = cat1("y_sample", 0).reshape(NCORE * NS, 1, D)
    return (y_p, y_s,
            stack1("p_lru_h"), stack1("p_lru_conv"), stack1("p_gdn_conv"),
            stack1("p_gdn_s"), stack1("p_hgrn_s"), stack1("p_ret_s"),
            stack1("p_mem_k").reshape(L, NCORE, NMEM, H, HD), stack1("p_mem_v").reshape(L, NCORE, NMEM, H, HD),
            cat1("s_lru_h", 1), cat1("s_lru_conv", 1), cat1("s_gdn_conv", 1),
            cat1("s_gdn_s", 1), cat1("s_hgrn_s", 1), cat1("s_ret_s", 1))
```

```python
import contextlib
import numpy as np
import concourse.bass as bass
import concourse.mybir as mybir
from concourse.bass_utils import run_bass_kernel_spmd

F32 = mybir.dt.float32
BF16 = mybir.dt.bfloat16
AF = mybir.ActivationFunctionType
ALU = mybir.AluOpType
AX = mybir.AxisListType

L = 2
D = 1024
T = 2048
TB = 512
NBLK = T // TB
NS = 16
NCORE = 8
W = 512
H = 4
HD = 128
DFF = 2816
NIN = 7176
NMEM = 256
PAST = 16384
ALPHA = (2 * L) ** 0.25
SCALE = HD ** -0.5
LN_EPS = 1e-5
NEPS = 1e-6
HC = 64
BIG = 30000.0
GAM = [1.0 - 2.0 ** (-5.0 - h) for h in range(H)]

SEG = dict(xa=(0, 512), gqkv=(512, 1536), gz=(2048, 512), gba=(2560, 8), hq=(2568, 512),
           hf=(3080, 512), hi=(3592, 512), hg=(4104, 512), rq=(4616, 512), rk=(5128, 512),
           rv=(5640, 512), rg=(6152, 512), xq=(6664, 512))

PROWS = [("lru_conv_w", 16), ("lru_conv_b", 4), ("lru_ba", 4), ("lru_bi", 4), ("lru_lambda", 4),
         ("gdn_conv_w", 48), ("gdn_norm_w", 4), ("lb0", 4), ("lb1", 4), ("hgrn_norm_w", 4),
         ("ret_gn_w", 4), ("ret_gn_b", 4), ("ln1_g", 8), ("ln1_b", 8), ("ln2_g", 8),
         ("ln2_b", 8), ("b_merge_gate", 40)]
PCOL = {}
_r = 0
for _n, _k in PROWS:
    PCOL[_n] = _r
    _r += _k
NPROW = _r


STAGELOG = []
import os as _osg
STRICT_SYNC = bool(int(_osg.environ.get('STRICT_SYNC', '1')))


class Sched:
    ENG = ("pe", "act", "dve", "pool", "sp")

    def __init__(self, nc, n_dma_sems=40):
        self.nc = nc
        self.e = {"pe": nc.tensor, "act": nc.scalar, "dve": nc.vector, "pool": nc.gpsimd, "sp": nc.sync}
        self.sem = {k: nc.alloc_semaphore(name="s_" + k) for k in self.ENG}
        self.cnt = {k: 0 for k in self.ENG}
        self.known = {k: {} for k in self.ENG}
        self.snaps = {k: {} for k in self.ENG}
        self.dsem = [nc.alloc_semaphore(name="d%d" % i) for i in range(n_dma_sems)]
        self.dcnt = [0] * n_dma_sems
        self.dpool = {"pool": list(range(0, 8)), "sp": list(range(8, 26)), "act": list(range(26, n_dma_sems))}
        self.dnext = {"pool": 0, "sp": 0, "act": 0}
        self.res = {}
        self.n_wait = 0
        self.n_ins = 0

    def _semh(self, sk):
        return self.dsem[sk[1]] if isinstance(sk, tuple) else self.sem[sk]

    def _wait(self, eng, tok, hazard="raw"):
        sk, v = tok
        if self.known[eng].get(sk, 0) >= v:
            return
        if sk == eng:
            if eng in ("pe", "sp"):
                return
            if not STRICT_SYNC and (hazard == "war" or v < self.cnt[eng]):
                return
        self.e[eng].wait_ge(self._semh(sk), v)
        self.n_wait += 1
        kn = self.known[eng]
        kn[sk] = v
        if not isinstance(sk, tuple):
            sn = self.snaps[sk].get(v)
            if sn:
                for k2, v2 in sn.items():
                    if kn.get(k2, 0) < v2:
                        kn[k2] = v2

    def _deps(self, eng, reads, writes):
        for r in reads:
            st = self.res.get(r)
            if st and st[0]:
                self._wait(eng, st[0], "raw")
        for r in writes:
            st = self.res.get(r)
            if st:
                if st[0]:
                    self._wait(eng, st[0], "waw")
                for sk, v in st[1].items():
                    self._wait(eng, (sk, v), "war")

    def _record(self, tok, reads, writes):
        for r in reads:
            st = self.res.setdefault(r, [None, {}])
            if st[1].get(tok[0], 0) < tok[1]:
                st[1][tok[0]] = tok[1]
        for r in writes:
            self.res[r] = [tok, {}]

    def op(self, eng, fn, reads=(), writes=()):
        self._deps(eng, reads, writes)
        ins = fn(self.e[eng])
        self.cnt[eng] += 1
        ins.then_inc(self.sem[eng], 1)
        tok = (eng, self.cnt[eng])
        self.snaps[eng][self.cnt[eng]] = dict(self.known[eng])
        self._record(tok, reads, writes)
        self.n_ins += 1
        return tok

    def dma(self, q, out, in_, reads=(), writes=(), **kw):
        pl = self.dpool[q]
        i = pl[self.dnext[q]]
        self.dnext[q] = (self.dnext[q] + 1) % len(pl)
        sk = ("d", i)
        if self.dcnt[i] > 0:
            self._wait(q, (sk, 16 * self.dcnt[i]))
        self._deps(q, reads, writes)
        ins = self.e[q].dma_start(out=out, in_=in_, **kw)
        self.dcnt[i] += 1
        ins.then_inc(self.dsem[i], 16)
        tok = (sk, 16 * self.dcnt[i])
        self._record(tok, reads, writes)
        self.n_ins += 1
        return tok

    def barrier(self):
        for eng in ("act", "dve", "sp"):
            for i, c in enumerate(self.dcnt):
                if c and i not in self.dpool["pool"]:
                    self._wait(eng, (("d", i), 16 * c))
            for k in ("pe", "act", "dve", "sp"):
                if k != eng and self.cnt[k]:
                    self._wait(eng, (k, self.cnt[k]))

    def finish(self, eng="sp"):
        for i, c in enumerate(self.dcnt):
            if c:
                self._wait(eng, (("d", i), 16 * c))
        for k in self.ENG:
            if k != eng and self.cnt[k]:
                self._wait(eng, (k, self.cnt[k]))


def _rn(ap):
    try:
        return ap.tensor.name
    except AttributeError:
        return ap.name


def _keys(ap):
    try:
        t = ap.tensor
    except AttributeError:
        return [ap.name]
    name = t.name
    shp = tuple(t.shape)
    if len(shp) < 3 or "DRam" in type(t).__name__:
        return [name]
    A = shp[1]
    G = 1
    for d_ in shp[2:]:
        G *= d_
    F = A * G
    pairs = list(ap.ap)
    lo = ap.offset % F
    ext = 1
    for st_, c_ in pairs[1:]:
        ext += (c_ - 1) * abs(st_)
    b0 = lo // G
    b1 = min(A - 1, (lo + ext - 1) // G)
    return [(name, b) for b in range(b0, b1 + 1)]


class KB:
    def __init__(self, nc):
        self.nc = nc
        self.S = Sched(nc)
        self.uid = 0
        self.stack = []
        self.caches = [{}]
        self.rr = 0

    def sb(self, name, shape, dt=F32):
        key = (name, tuple(shape), str(dt))
        cache = self.caches[-1]
        if key in cache:
            return cache[key]
        self.uid += 1
        t = self.stack[-1].enter_context(self.nc.sbuf_tensor("%s_%d" % (name, self.uid), list(shape), dt))
        cache[key] = t
        return t

    @contextlib.contextmanager
    def scope(self):
        es = contextlib.ExitStack()
        self.stack.append(es)
        self.caches.append({})
        try:
            yield
        finally:
            self.S.barrier()
            self.stack.pop()
            self.caches.pop()
            es.close()

    def bank(self):
        b = self.ps[self.rr]
        self.rr = (self.rr + 1) % self.n_rot
        return b

    def _rw(self, ins, outs):
        r = []
        for a in ins:
            if not isinstance(a, (int, float)) and a is not None:
                r += _keys(a)
        w = []
        for a in outs:
            w += _keys(a)
        w = w + [n for n in r if isinstance(n, str) and n.startswith("ps") and n not in w]
        return r, w

    def mm(self, out, lhsT, rhs, start=True, stop=True):
        r, w = self._rw([lhsT, rhs], [out])
        self.S.op("pe", lambda e: e.matmul(out, lhsT, rhs, start=start, stop=stop), r, w)

    def tr(self, out, in_, ident):
        r, w = self._rw([in_, ident], [out])
        self.S.op("pe", lambda e: e.transpose(out, in_, ident), r, w)

    def act(self, out, in_, func, bias=0.0, scale=1.0, accum=None, eng="act"):
        r, w = self._rw([in_, bias, scale], [out] + ([accum] if accum is not None else []))
        if accum is None:
            self.S.op("act", lambda e: e.activation(out, in_, func, bias=bias, scale=scale), r, w)
        else:
            self.S.op("act", lambda e: e.activation(out, in_, func, bias=bias, scale=scale, accum_out=accum), r, w)

    def cp(self, out, in_, eng="act"):
        r, w = self._rw([in_], [out])
        if eng == "act":
            self.S.op("act", lambda e: e.copy(out, in_), r, w)
        else:
            self.S.op(eng, lambda e: e.tensor_copy(out, in_), r, w)

    def tt(self, out, a, b, op, eng="dve"):
        r, w = self._rw([a, b], [out])
        self.S.op(eng, lambda e: e.tensor_tensor(out, a, b, op), r, w)

    def ts(self, out, a, s1, op0, s2=None, op1=None, eng="dve"):
        r, w = self._rw([a, s1, s2], [out])
        if op1 is None:
            self.S.op(eng, lambda e: e.tensor_scalar(out, a, s1, None, op0), r, w)
        else:
            self.S.op(eng, lambda e: e.tensor_scalar(out, a, s1, s2, op0, op1), r, w)

    def stt(self, out, in0, scalar, in1, op0, op1):
        r, w = self._rw([in0, scalar, in1], [out])
        self.S.op("dve", lambda e: e.scalar_tensor_tensor(out, in0, scalar, in1, op0, op1), r, w)

    def scan(self, out, d0, d1, init, op0, op1):
        r, w = self._rw([d0, d1, init], [out])
        self.S.op("dve", lambda e: e.tensor_tensor_scan(out, d0, d1, init, op0, op1), r, w)

    def recip(self, out, in_):
        r, w = self._rw([in_], [out])
        self.S.op("dve", lambda e: e.reciprocal(out, in_), r, w)

    def rsum(self, out, in_):
        r, w = self._rw([in_], [out])
        self.S.op("dve", lambda e: e.reduce_sum(out, in_, AX.X), r, w)

    def memset(self, ap, v, eng="dve"):
        r, w = self._rw([], [ap])
        self.S.op(eng, lambda e: e.memset(ap, v), r, w)

    def dma(self, out, in_, q="sp", **kw):
        r, w = self._rw([in_], [out])
        self.S.dma(q, out, in_, r, w, **kw)


def build_program():
    nc = bass.Bass("TRN2", target_bir_lowering=False)
    K = KB(nc)

    def din(name, shape):
        return nc.dram_tensor(name, list(shape), F32, kind="ExternalInput").ap()

    def dout(name, shape):
        return nc.dram_tensor(name, list(shape), F32, kind="ExternalOutput").ap()

    x_prompt = din("x_prompt", [T, D])
    x_sample = din("x_sample", [NS, D])
    mem_prompt = din("mem_prompt", [NMEM, D])
    st_lru_h = din("state_lru_h", [L, NS, W])
    st_lru_conv = din("state_lru_conv", [L, NS, 3, W])
    st_gdn_conv = din("state_gdn_conv", [L, NS, 3, 3 * W])
    st_gdn_s = din("state_gdn_s", [L, NS, H, HD, HD])
    st_hgrn_s = din("state_hgrn_s", [L, NS, H, HD, HD])
    st_ret_s = din("state_ret_s", [L, NS, H, HD, HD])
    cache_k = din("cache_mem_k", [L, NS, NMEM, W])
    cache_v = din("cache_mem_v", [L, NS, NMEM, W])
    w_in = din("w_in", [L, D, NIN])
    lru_conv_w = din("lru_conv_w", [L, 4, W])
    lru_conv_b = din("lru_conv_b", [L, W])
    lru_wa = din("lru_wa", [L, H, HD, HD])
    lru_ba = din("lru_ba", [L, W])
    lru_wi = din("lru_wi", [L, H, HD, HD])
    lru_bi = din("lru_bi", [L, W])
    lru_lambda = din("lru_lambda", [L, W])
    gdn_conv_w = din("gdn_conv_w", [L, 4, 3 * W])
    gdn_a_log = din("gdn_a_log", [L, H])
    gdn_dt_bias = din("gdn_dt_bias", [L, H])
    gdn_norm_w = din("gdn_norm_w", [L, W])
    hgrn_lb_raw = din("hgrn_lb_raw", [L, W])
    hgrn_norm_w = din("hgrn_norm_w", [L, W])
    ret_gn_w = din("ret_gn_w", [L, W])
    ret_gn_b = din("ret_gn_b", [L, W])
    w_mem_k = din("w_mem_k", [L, D, W])
    w_mem_v = din("w_mem_v", [L, D, W])
    w_merge_gate = din("w_merge_gate", [L, 5, D, D])
    b_merge_gate = din("b_merge_gate", [L, 5, D])
    w_branch = din("w_branch", [L, 5, W, D])
    w_out = din("w_out", [L, D, D])
    ln1_g = din("ln1_g", [L, D])
    ln1_b = din("ln1_b", [L, D])
    w_ffn_up = din("w_ffn_up", [L, D, 2 * DFF])
    w_ffn_down = din("w_ffn_down", [L, DFF, D])
    ln2_g = din("ln2_g", [L, D])
    ln2_b = din("ln2_b", [L, D])
    c_ident = din("c_ident", [128, 128])
    c_uincl = din("c_uincl", [128, 128])
    c_ugt = din("c_ugt", [128, 128])
    c_pswap = din("c_pswap", [128, 128])
    c_posm = din("c_posm", [128, 128])
    c_negm = din("c_negm", [128, 128])
    c_dmask = din("c_dmask", [128, H * 128])
    c_decq = din("c_decq", [128, H * 128])
    c_dendr = din("c_dendr", [128, H * 128])
    c_cos = din("c_cos", [128, T])
    c_sin = din("c_sin", [128, T])
    c_cs_s = din("c_cs_s", [128, 2])
    c_reset = din("c_reset", [128, TB])

    y_prompt = dout("y_prompt", [T, D])
    y_sample = dout("y_sample", [NS, D])
    o_p_lru_h = dout("p_lru_h", [L, W])
    o_p_lru_conv = dout("p_lru_conv", [L, 3, W])
    o_p_gdn_conv = dout("p_gdn_conv", [L, 3, 3 * W])
    o_p_gdn_s = dout("p_gdn_s", [L, H, HD, HD])
    o_p_hgrn_s = dout("p_hgrn_s", [L, H, HD, HD])
    o_p_ret_s = dout("p_ret_s", [L, H, HD, HD])
    o_p_mem_k = dout("p_mem_k", [L, NMEM, W])
    o_p_mem_v = dout("p_mem_v", [L, NMEM, W])
    o_s_lru_h = dout("s_lru_h", [L, NS, W])
    o_s_lru_conv = dout("s_lru_conv", [L, NS, 3, W])
    o_s_gdn_conv = dout("s_gdn_conv", [L, NS, 3, 3 * W])
    o_s_gdn_s = dout("s_gdn_s", [L, NS, H, HD, HD])
    o_s_hgrn_s = dout("s_hgrn_s", [L, NS, H, HD, HD])
    o_s_ret_s = dout("s_ret_s", [L, NS, H, HD, HD])

    root = contextlib.ExitStack()
    K.stack.append(root)
    K.ps = [root.enter_context(nc.psum_tensor("ps%d" % i, [128, 512], F32)) for i in range(8)]
    K.n_rot = 4
    PD = K.ps[4:8]

    ident = K.sb("ident", [128, 128])
    ones = K.sb("ones", [128, 128])
    onesb = K.sb("onesb", [128, 128], BF16)
    uincl = K.sb("uincl", [128, 128])
    ugt = K.sb("ugt", [128, 128])
    pswap = K.sb("pswap", [128, 128])
    posm = K.sb("posm", [128, 128])
    negm = K.sb("negm", [128, 128])
    dmask = K.sb("dmask", [128, H, 128])
    decq = K.sb("decq", [128, H, 128])
    dendr = K.sb("dendr", [128, H, 128])
    cs_s = K.sb("cs_s", [128, 2])
    resetm = K.sb("resetm", [128, TB])
    for t_, d_ in ((ident, c_ident), (uincl, c_uincl), (ugt, c_ugt), (pswap, c_pswap), (posm, c_posm),
                   (negm, c_negm), (cs_s, c_cs_s), (resetm, c_reset)):
        K.dma(t_[:], d_)
    for t_, d_ in ((dmask, c_dmask), (decq, c_decq), (dendr, c_dendr)):
        K.dma(t_[:], d_.rearrange("p (h c) -> p h c", h=H))
    K.memset(ones[:], 1.0)
    K.memset(onesb[:], 1.0)
    sel8 = K.sb("sel8", [8, 8, 128])
    for r_ in range(8):
        K.ts(sel8[:, r_, :], ones[0:8, :], ident[0:8, r_:r_ + 1], ALU.mult)

    NWB = 4
    wbuf = [K.sb("wb%d" % i, [128, 4096], BF16) for i in range(NWB)]
    wstate = {"i": 0}

    NWT = 50
    wscr = [nc.dram_tensor("wscr%d" % l, [NWT, 128, 4096], BF16).ap() for l in range(L)]
    cur = {"blk": 0, "l": 0, "widx": 0, "pend": None}

    def wflush():
        if cur["pend"] is not None:
            dst, srcv = cur["pend"]
            K.dma(dst, srcv, q="pool")
            cur["pend"] = None

    def wload(src2d, nk, ncols, cache=True):
        b = wbuf[wstate["i"]]
        wstate["i"] = (wstate["i"] + 1) % NWB
        flat = b[:, 0:nk * ncols]
        v = flat.rearrange("p (k c) -> p k c", k=nk)
        if not cache:
            K.dma(v, src2d.rearrange("(k p) c -> p k c", p=128), q="pool")
            return v
        idx = cur["widx"]
        cur["widx"] += 1
        assert idx < NWT
        scr = wscr[cur["l"]][idx][:, 0:nk * ncols]
        if cur["blk"] == 0:
            K.dma(v, src2d.rearrange("(k p) c -> p k c", p=128), q="pool")
            wflush()
            cur["pend"] = (scr, flat)
        else:
            K.dma(flat, scr, q="pool")
        return v

    Sst = {}
    for l in range(L):
        for br in ("gdn", "hgrn", "ret"):
            Sst[(l, br)] = K.sb("S_%s%d" % (br, l), [128, H, 128])
            K.memset(Sst[(l, br)][:], 0.0)
    hcar = [K.sb("hcar%d" % l, [128, 4]) for l in range(L)]
    halo_a = [K.sb("haloa%d" % l, [128, 4, 3]) for l in range(L)]
    halo_g = [K.sb("halog%d" % l, [128, 12, 3]) for l in range(L)]
    for l in range(L):
        K.memset(hcar[l][:], 0.0)
        K.memset(halo_a[l][:], 0.0)
        K.memset(halo_g[l][:], 0.0)
    memKT = [K.sb("memKT%d" % l, [128, H, NMEM], BF16) for l in range(L)]
    memV = [K.sb("memV%d" % l, [128, 2, W], BF16) for l in range(L)]
    xmT = K.sb("xmT", [128, 8, NMEM], BF16)
    pc = [K.sb("pc%d" % l, [128, NPROW + 8]) for l in range(L)]
    wa_bf = [K.sb("wa%d" % l, [128, H, 128], BF16) for l in range(L)]
    wi_bf = [K.sb("wi%d" % l, [128, H, 128], BF16) for l in range(L)]
    gbc = [K.sb("gbc%d" % l, [128, 8]) for l in range(L)]
    der = [K.sb("der%d" % l, [128, 16]) for l in range(L)]

    class Grp:
        pass

    P = Grp()
    P.N = TB
    P.prompt = True
    P.xf = K.sb("p_xf", [128, 8, TB])
    P.xb = K.sb("p_xb", [128, 8, TB], BF16)
    Sg = Grp()
    Sg.N = NS
    Sg.prompt = False
    Sg.xf = K.sb("s_xf", [128, 8, NS])
    Sg.xb = K.sb("s_xb", [128, 8, NS], BF16)

    import os as _os0
    with K.scope():
        ptm = K.sb("ptm", [128, 2, 128])
        K.memset(ptm[:], 0.0)
        for l in range(L if not (int(_os0.environ.get("DBGX", "0")) & 4) else 0):
            def prow(name, src2d):
                r0 = PCOL[name]
                n = src2d.shape[0]
                K.dma(ptm[r0 % 128:r0 % 128 + n, r0 // 128, :], src2d)
            prow("lru_conv_w", lru_conv_w[l].rearrange("t (c p) -> (t c) p", p=128))
            prow("lru_conv_b", lru_conv_b[l].rearrange("(c p) -> c p", p=128))
            prow("lru_ba", lru_ba[l].rearrange("(c p) -> c p", p=128))
            prow("lru_bi", lru_bi[l].rearrange("(c p) -> c p", p=128))
            prow("lru_lambda", lru_lambda[l].rearrange("(c p) -> c p", p=128))
            prow("gdn_conv_w", gdn_conv_w[l].rearrange("t (c p) -> (t c) p", p=128))
            prow("gdn_norm_w", gdn_norm_w[l].rearrange("(c p) -> c p", p=128))
            prow("lb0", hgrn_lb_raw[0].rearrange("(c p) -> c p", p=128))
            prow("lb1", hgrn_lb_raw[1].rearrange("(c p) -> c p", p=128))
            prow("hgrn_norm_w", hgrn_norm_w[l].rearrange("(c p) -> c p", p=128))
            prow("ret_gn_w", ret_gn_w[l].rearrange("(c p) -> c p", p=128))
            prow("ret_gn_b", ret_gn_b[l].rearrange("(c p) -> c p", p=128))
            prow("ln1_g", ln1_g[l].rearrange("(c p) -> c p", p=128))
            prow("ln1_b", ln1_b[l].rearrange("(c p) -> c p", p=128))
            prow("ln2_g", ln2_g[l].rearrange("(c p) -> c p", p=128))
            prow("ln2_b", ln2_b[l].rearrange("(c p) -> c p", p=128))
            prow("b_merge_gate", b_merge_gate[l].rearrange("n (c p) -> (n c) p", p=128))
            pb = K.bank()
            K.tr(pb[:, 0:128], ptm[:, 0, :], ident[:])
            K.tr(pb[:, 128:128 + 48], ptm[0:48, 1, :], ident[0:48, 0:48])
            K.cp(pc[l][:, 0:NPROW], pb[:, 0:NPROW])
            lam = pc[l][:, PCOL["lru_lambda"]:PCOL["lru_lambda"] + 4]
            tmp = K.sb("tmpd", [128, 4])
            K.act(tmp[:], lam, AF.Exp, scale=-1.0)
            K.act(tmp[:], tmp[:], AF.Ln, bias=1.0)
            K.ts(der[l][:, 0:4], tmp[:], -8.0, ALU.mult)
            K.ts(der[l][:, 4:8], tmp[:], -16.0, ALU.mult)
            if l == 0:
                K.memset(der[l][:, 8:12], 0.0)
                K.memset(der[l][:, 12:16], 1.0)
            else:
                d_ = K.sb("tmpd2", [128, 4])
                K.tt(d_[:], pc[l][:, PCOL["lb1"]:PCOL["lb1"] + 4], pc[l][:, PCOL["lb0"]:PCOL["lb0"] + 4], ALU.subtract)
                K.act(der[l][:, 8:12], d_[:], AF.Sigmoid)
                K.ts(der[l][:, 12:16], der[l][:, 8:12], -1.0, ALU.mult, 1.0, ALU.add)
            K.dma(wa_bf[l][:], lru_wa[l].rearrange("h i j -> i h j"), q="pool")
            K.dma(wi_bf[l][:], lru_wi[l].rearrange("h i j -> i h j"), q="pool")
            K.dma(gbc[l][:, 0:4], gdn_a_log[l:l + 1, :].partition_broadcast(128))
            K.dma(gbc[l][:, 4:8], gdn_dt_bias[l:l + 1, :].partition_broadcast(128))
            K.act(gbc[l][:, 0:4], gbc[l][:, 0:4], AF.Exp)
            K.ts(gbc[l][:, 0:4], gbc[l][:, 0:4], -1.0, ALU.mult)

    def pcol(l, name, i=0):
        c = PCOL[name] + i
        return pc[l][:, c:c + 1]

    def project(l, seg, groups, handler):
        c0, ncol_tot = SEG[seg]
        K.n_rot = 8
        try:
            _project(l, seg, groups, handler)
        finally:
            K.n_rot = 4
            K.rr = 0

    def _project(l, seg, groups, handler):
        c0, ncol_tot = SEG[seg]
        for cc in range(0, ncol_tot, 512):
            ncol = min(512, ncol_tot - cc)
            wt = wload(w_in[l, :, c0 + cc:c0 + cc + ncol], 8, ncol)
            for g in groups:
                for j in range(ncol // 128):
                    p = K.bank()
                    for k in range(8):
                        K.mm(p[:, :g.N], wt[:, k, j * 128:(j + 1) * 128], g.xb[:, k, :g.N], start=(k == 0), stop=(k == 7))
                    handler(g, cc // 128 + j, p[:, :g.N])

    def ln_stat_step(g, zf, c, pm_ap, pq_ap):
        zsq = K.sb("zsq", [128, 8, g.N])
        K.act(zsq[:, c, :], zf[:, c, :], AF.Square)
        K.mm(pm_ap, ones[:], zf[:, c, :], start=(c == 0), stop=(c == 7))
        K.mm(pq_ap, ones[:], zsq[:, c, :], start=(c == 0), stop=(c == 7))

    def layernorm(l, g, zf, gname, bname, stats=None):
        N = g.N
        zsq = K.sb("zsq", [128, 8, N])
        if stats is None:
            pm = K.bank()[:, :N]
            pq = K.bank()[:, :N]
            for c in range(8):
                ln_stat_step(g, zf, c, pm, pq)
        else:
            pm, pq = stats
        mu = K.sb("mu", [128, N])
        K.act(mu[:], pm, AF.Copy, scale=1.0 / D)
        musq = K.sb("musq", [128, N])
        K.act(musq[:], mu[:], AF.Square)
        var = K.sb("var", [128, N])
        K.stt(var[:], pq, 1.0 / D, musq[:], ALU.mult, ALU.subtract)
        K.act(var[:], var[:], AF.Ln, bias=LN_EPS)
        rstd = K.sb("rstd", [128, N])
        K.act(rstd[:], var[:], AF.Exp, scale=-0.5)
        for c in range(8):
            t = zsq[:, c, :]
            en = "dve"
            K.tt(t, zf[:, c, :], mu[:], ALU.subtract, eng=en)
            K.tt(t, t, rstd[:], ALU.mult, eng=en)
            K.ts(g.xf[:, c, :N], t, pcol(l, gname, c), ALU.mult, pcol(l, bname, c), ALU.add, eng=en)
            K.cp(g.xb[:, c, :N], g.xf[:, c, :N])

    def head_rms(l, g, po, wname, h, gate, yout):
        N = g.N
        osq = K.sb("osq%d" % (h % 2), [128, N])
        K.act(osq[:], po, AF.Square)
        pq = K.bank()
        K.mm(pq[:, :N], ones[:], osq[:])
        sd = K.sb("sd%d" % (h % 2), [128, N])
        K.act(sd[:], pq[:, :N], AF.Ln, bias=NEPS, scale=1.0 / HD)
        K.act(sd[:], sd[:], AF.Exp, scale=-0.5)
        K.stt(osq[:], po, pcol(l, wname, h), sd[:], ALU.mult, ALU.mult)
        K.tt(yout, osq[:], gate, ALU.mult)

    def head_gn(l, g, po, h, gate, yout):
        N = g.N
        osb = K.sb("osb%d" % (h % 2), [128, N])
        K.cp(osb[:], po)
        osq = K.sb("osq2%d" % (h % 2), [128, N])
        K.act(osq[:], po, AF.Square)
        pm = K.bank()
        pq = K.bank()
        K.mm(pm[:, :N], ones[:], osb[:])
        K.mm(pq[:, :N], ones[:], osq[:])
        mu = K.sb("gmu%d" % (h % 2), [128, N])
        K.act(mu[:], pm[:, :N], AF.Copy, scale=1.0 / HD)
        K.act(osq[:], mu[:], AF.Square)
        var = K.sb("gvar%d" % (h % 2), [128, N])
        K.stt(var[:], pq[:, :N], 1.0 / HD, osq[:], ALU.mult, ALU.subtract)
        K.act(var[:], var[:], AF.Ln, bias=NEPS)
        K.act(var[:], var[:], AF.Exp, scale=-0.5)
        K.tt(osb[:], osb[:], mu[:], ALU.subtract)
        K.tt(osb[:], osb[:], var[:], ALU.mult)
        K.ts(osb[:], osb[:], pcol(l, "ret_gn_w", h), ALU.mult, pcol(l, "ret_gn_b", h), ALU.add)
        K.tt(yout, osb[:], gate, ALU.mult)

    def branch_lru(l, blk, groups):
        with K.scope():
            for g in groups:
                N = g.N
                if g.prompt:
                    g.xa = K.sb("xa_h", [128, 4, 3 + N])
                    K.cp(g.xa[:, :, 0:3], halo_a[l][:], eng="dve")
                else:
                    g.xa = K.sb("xa_s", [128, 4, N, 4])
                    g.h0 = K.sb("h0_s", [128, 4, N])
                    stc = K.sb("stc", [48, W])
                    K.dma(stc[:], st_lru_conv[l].rearrange("b r c -> (b r) c"))
                    sth = K.sb("sth", [NS, W])
                    K.dma(sth[:], st_lru_h[l])
                    for c in range(4):
                        pb = K.bank()
                        K.tr(pb[:, 0:48], stc[:, c * 128:(c + 1) * 128], ident[0:48, 0:48])
                        K.tr(pb[:, 64:64 + NS], sth[:, c * 128:(c + 1) * 128], ident[0:NS, 0:NS])
                        K.cp(g.xa[:, c, :, 0:3], pb[:, 0:48].rearrange("p (b r) -> p b r", r=3))
                        K.cp(g.h0[:, c, :], pb[:, 64:64 + NS])

            def ev(g, j, p):
                N = g.N
                if g.prompt:
                    K.cp(g.xa[:, j, 3:3 + N], p)
                    tp_ = lambda t: g.xa[:, j, t:t + N]
                else:
                    K.cp(g.xa[:, j, :, 3], p)
                    tp_ = lambda t: g.xa[:, j, :, t]
                xc = K.sb("xc%d" % j, [128, N])
                K.ts(xc[:], tp_(3), pcol(l, "lru_conv_w", 12 + j), ALU.mult, pcol(l, "lru_conv_b", j), ALU.add)
                for t in (2, 1, 0):
                    K.stt(xc[:], tp_(t), pcol(l, "lru_conv_w", 4 * t + j), xc[:], ALU.mult, ALU.add)
                xcb = K.sb("xcb%d" % j, [128, N], BF16)
                K.cp(xcb[:], xc[:], eng="dve")
            project(l, "xa", groups, ev)
            for g in groups:
                N = g.N
                if g.prompt:
                    K.cp(halo_a[l][:], g.xa[:, :, N:N + 3], eng="dve")
                    tap = lambda j, t: g.xa[:, j, t:t + N]
                else:
                    tap = lambda j, t: g.xa[:, j, :, t]
                lru_rg = []
                for j in range(4):
                    xc = K.sb("xc%d" % j, [128, N])
                    xcb = K.sb("xcb%d" % j, [128, N], BF16)
                    p1 = K.bank()
                    p2 = K.bank()
                    K.mm(p1[:, :N], wa_bf[l][:, j, :], xcb[:])
                    K.mm(p2[:, :N], wi_bf[l][:, j, :], xcb[:])
                    r_ = K.sb("lr%d" % j, [128, N])
                    ig = K.sb("lig%d" % j, [128, N])
                    K.act(r_[:], p1[:, :N], AF.Sigmoid, bias=pcol(l, "lru_ba", j))
                    K.act(ig[:], p2[:, :N], AF.Sigmoid, bias=pcol(l, "lru_bi", j))
                    lru_rg.append((xc, r_, ig))
                for j in range(4):
                    xc, r_, ig = lru_rg[j]
                    a_ = K.sb("la%d" % (j % 2), [128, N])
                    a2 = K.sb("la2%d" % (j % 2), [128, N])
                    K.act(a_[:], r_[:], AF.Exp, scale=der[l][:, j:j + 1])
                    K.act(a2[:], r_[:], AF.Exp, scale=der[l][:, 4 + j:5 + j])
                    K.act(a2[:], a2[:], AF.Ln, bias=1.0, scale=-1.0)
                    K.act(a2[:], a2[:], AF.Exp, scale=0.5)
                    K.tt(ig[:], ig[:], xc[:], ALU.mult)
                    K.tt(ig[:], ig[:], a2[:], ALU.mult)
                    hh = K.sb("lh%d" % (j % 2), [128, N])
                    if g.prompt:
                        K.scan(hh[:], a_[:], ig[:], hcar[l][:, j:j + 1], ALU.mult, ALU.add)
                        K.cp(hcar[l][:, j:j + 1], hh[:, N - 1:N], eng="dve")
                    else:
                        K.tt(hh[:], a_[:], g.h0[:, j, :], ALU.mult)
                        K.tt(hh[:], hh[:], ig[:], ALU.add)
                        K.cp(g.h0[:, j, :], hh[:], eng="dve")
                    K.cp(g.y[0][:, j, :], hh[:])
                if g.prompt:
                    if blk == NBLK - 1:
                        K.dma(o_p_lru_h[l].rearrange("(c p) -> p c", p=128), hcar[l][:], q="act", allow_slow_non_contiguous=True)
                        for j in range(4):
                            K.dma(o_p_lru_conv[l][:, j * 128:(j + 1) * 128].rearrange("r p -> p r"), halo_a[l][:, j, :], q="act",
                                  allow_slow_non_contiguous=True)
                else:
                    ost = K.sb("ost", [NS, 2, W])
                    pb = K.bank()
                    pb2 = K.bank()
                    for c in range(4):
                        K.tr(pb[0:NS, c * 128:(c + 1) * 128], g.h0[:, c, :], ident[:])
                        K.tr(pb2[0:NS, c * 128:(c + 1) * 128], g.xa[:, c, :, 3], ident[:])
                    K.cp(ost[:, 0, :], pb[0:NS, :])
                    K.cp(ost[:, 1, :], pb2[0:NS, :])
                    K.dma(o_s_lru_h[l], ost[:, 0, :], q="act")
                    K.dma(o_s_lru_conv[l][:, 2, :], ost[:, 1, :], q="act")
                    K.dma(o_s_lru_conv[l][:, 0:2, :], st_lru_conv[l][:, 1:3, :], q="act")


    def branch_gdn(l, blk, groups):
        with K.scope():
            for g in groups:
                N = g.N
                g.gz = K.sb("gz", [128, 4, N], BF16)
                if g.prompt:
                    g.gq = K.sb("gq_h", [128, 12, 3 + N])
                    K.cp(g.gq[:, :, 0:3], halo_g[l][:], eng="dve")
                    g.gba = K.sb("gba_tm", [128, 4, 8])
                else:
                    g.gq = K.sb("gq_s", [128, 12, N, 4])
                    g.gs = K.sb("gqs_s", [128, 12, N])
                    g.gba = K.sb("gba_fm", [8, N])
                    for part in range(3):
                        stc = K.sb("stcg", [48, W])
                        K.dma(stc[:], st_gdn_conv[l][:, :, part * W:(part + 1) * W].rearrange("b r c -> (b r) c"))
                        for c in range(4):
                            pb = K.bank()
                            K.tr(pb[:, 0:48], stc[:, c * 128:(c + 1) * 128], ident[0:48, 0:48])
                            K.cp(g.gq[:, part * 4 + c, :, 0:3], pb[:, 0:48].rearrange("p (b r) -> p b r", r=3))

            def ev_qkv(g, j, p):
                N = g.N
                if g.prompt:
                    K.cp(g.gq[:, j, 3:3 + N], p)
                    K.cp(halo_g[l][:, j, :], g.gq[:, j, N:N + 3], eng="dve")
                    tap = lambda t: g.gq[:, j, t:t + N]
                else:
                    K.cp(g.gq[:, j, :, 3], p)
                    tap = lambda t: g.gq[:, j, :, t]
                xc = K.sb("gxc%d" % (j % 2), [128, N])
                K.ts(xc[:], tap(3), pcol(l, "gdn_conv_w", 36 + j), ALU.mult)
                for t in (2, 1, 0):
                    K.stt(xc[:], tap(t), pcol(l, "gdn_conv_w", 12 * t + j), xc[:], ALU.mult, ALU.add)
                if g.prompt:
                    if blk == NBLK - 1:
                        K.dma(o_p_gdn_conv[l][:, j * 128:(j + 1) * 128].rearrange("r p -> p r"), halo_g[l][:, j, :], q="act",
                              allow_slow_non_contiguous=True)
                    fin = lambda: K.act(g.gq[:, j, 3:3 + N], xc[:], AF.Silu)
                else:
                    fin = lambda: K.act(g.gs[:, j, :], xc[:], AF.Silu)
                prev = silu_pend.pop(id(g), None)
                if prev is not None:
                    prev()
                silu_pend[id(g)] = fin
            silu_pend = {}
            project(l, "gqkv", groups, ev_qkv)
            for fin_ in list(silu_pend.values()):
                fin_()
            silu_pend.clear()

            def ev_z(g, j, p):
                K.act(g.gz[:, j, :], p, AF.Silu)
            project(l, "gz", groups, ev_z)
            wba = wload(w_in[l, :, 2560:2568], 8, 8)
            for g in groups:
                if g.prompt:
                    for tt in range(4):
                        p = K.bank()
                        for k in range(8):
                            K.mm(p[:, 0:8], g.xb[:, k, tt * 128:(tt + 1) * 128], wba[:, k, :], start=(k == 0), stop=(k == 7))
                        K.cp(g.gba[:, tt, :], p[:, 0:8])
                else:
                    p = K.bank()
                    for k in range(8):
                        K.mm(p[0:8, :g.N], wba[:, k, :], g.xb[:, k, :g.N], start=(k == 0), stop=(k == 7))
                    K.cp(g.gba[:], p[0:8, :g.N])
            for g in groups:
                with K.scope():
                    if g.prompt:
                        gdn_prompt(l, blk, g)
                    else:
                        gdn_sample_pre(l, g)
                        gdn_sample_step(l, g)

    def gdn_prompt(l, blk, g):
        N = g.N
        S_ = Sst[(l, "gdn")]
        HS = range(H)
        betaA = K.sb("gbetaA", [128, 16])
        nbetaA = K.sb("gnbetaA", [128, 16])
        ggA = K.sb("gggA", [128, 16])
        GcA = K.sb("gGcA", [128, 2, 16])
        eGA = K.sb("geGA", [128, 2, 16])
        nGA = K.sb("gnGA", [128, 16])
        K.act(betaA[:].rearrange("p (t h) -> p t h", h=4), g.gba[:, :, 0:4], AF.Exp, scale=-1.0)
        K.ts(betaA[:], betaA[:], 1.0, ALU.add)
        K.recip(betaA[:], betaA[:])
        K.ts(nbetaA[:], betaA[:], -1.0, ALU.mult)
        for tt in range(4):
            K.tt(ggA[:, tt * 4:(tt + 1) * 4], g.gba[:, tt, 4:8], gbc[l][:, 4:8], ALU.add)
        K.act(ggA[:], ggA[:], AF.Exp)
        K.act(ggA[:], ggA[:], AF.Ln, bias=1.0)
        for tt in range(4):
            K.tt(ggA[:, tt * 4:(tt + 1) * 4], ggA[:, tt * 4:(tt + 1) * 4], gbc[l][:, 0:4], ALU.mult)
        pG = K.bank()
        K.mm(pG[:, 0:16], uincl[:], ggA[:])
        K.mm(pG[:, 16:32], ugt[:], ggA[:])
        K.cp(GcA[:].rearrange("p a c -> p (a c)"), pG[:, 0:32])
        K.act(eGA[:], GcA[:], AF.Exp)
        K.ts(nGA[:], GcA[:, 0, :], -1.0, ALU.mult)
        ssqA = K.sb("gssqA", [128, 32])
        rsA = K.sb("grsA", [128, 32])
        pS = K.bank()
        for j in range(8):
            sq = K.sb("gsqb", [128, N], BF16)
            K.act(sq[:], g.gq[:, j, 3:3 + N], AF.Square)
            for tt in range(4):
                K.mm(pS[:, tt * 8 + j:tt * 8 + j + 1], sq[:, tt * 128:(tt + 1) * 128], onesb[:, 0:1])
        K.act(ssqA[:], pS[:, 0:32], AF.Ln, bias=NEPS)
        K.act(rsA[:], ssqA[:], AF.Exp, scale=-0.5)
        for tt in range(4):
            c0 = 3 + tt * 128
            ts_ = slice(tt * 4, (tt + 1) * 4)
            beta = betaA[:, ts_]
            nbeta = nbetaA[:, ts_]
            Gc = GcA[:, 0, ts_]
            eG = eGA[:, 0, ts_]
            eE = eGA[:, 1, ts_]
            nG = nGA[:, ts_]
            rs = rsA[:, tt * 8:(tt + 1) * 8]
            qkv = [K.sb("qkv%d" % h, [128, 384]) for h in HS]
            for h in HS:
                p = K.bank()
                for i_ in range(3):
                    K.tr(p[:, i_ * 128:(i_ + 1) * 128], g.gq[:, i_ * 4 + h, c0:c0 + 128], ident[:])
                K.cp(qkv[h][:], p[:, 0:384])
            tm = [K.sb("gtm%d" % h, [128, 4, 128]) for h in HS]
            Y = [K.sb("gY%d" % h, [128, 256]) for h in HS]
            for h in HS:
                q_, k_, v_ = qkv[h][:, 0:128], qkv[h][:, 128:256], qkv[h][:, 256:384]
                K.ts(tm[h][:, 0, :], k_, rs[:, 4 + h:5 + h], ALU.mult)
                K.ts(tm[h][:, 1, :], q_, rs[:, h:h + 1], ALU.mult, SCALE, ALU.mult)
                K.ts(tm[h][:, 2, :], tm[h][:, 1, :], eG[:, h:h + 1], ALU.mult)
                K.ts(tm[h][:, 3, :], tm[h][:, 0, :], eE[:, h:h + 1], ALU.mult)
                K.ts(Y[h][:, 0:128], v_, beta[:, h:h + 1], ALU.mult)
                K.ts(Y[h][:, 128:256], tm[h][:, 0, :], beta[:, h:h + 1], ALU.mult, eG[:, h:h + 1], ALU.mult)
            DEC = [K.sb("gDEC%d" % h, [128, 256]) for h in HS]
            dS = K.sb("gdS", [128, 4])
            for h in HS:
                dg = K.sb("gdg", [128, 128])
                K.ts(dg[:], ident[:], Gc[:, h:h + 1], ALU.mult)
                pR = K.bank()
                K.mm(pR[:, 0:128], ones[:], dg[:])
                t1 = K.sb("gt1", [128, 256])
                K.tt(t1[:, 0:128], pR[:, 0:128], posm[:], ALU.add)
                K.tt(t1[:, 128:256], pR[:, 0:128], negm[:], ALU.add)
                K.act(DEC[h][:, 0:128], t1[:, 0:128], AF.Exp, bias=Gc[:, h:h + 1], scale=-1.0)
                K.act(DEC[h][:, 128:256], t1[:, 128:256], AF.Exp, bias=nG[:, h:h + 1], scale=1.0)
                K.act(dS[:, h:h + 1], pR[:, 127:128], AF.Exp)
            fmT = [K.sb("gfm%d" % h, [128, 384]) for h in HS]
            for h in HS:
                p = K.bank()
                for i_ in range(3):
                    K.tr(p[:, i_ * 128:(i_ + 1) * 128], tm[h][:, i_, :], ident[:])
                K.cp(fmT[h][:], p[:, 0:384])
            X = [[K.sb("gX%d_0" % h, [128, 256]), DEC[h]] for h in HS]
            AQ = [K.sb("gAQ%d" % h, [128, 128]) for h in HS]
            for h in HS:
                p = K.bank()
                K.mm(p[:, 0:128], fmT[h][:, 0:128], fmT[h][:, 0:128])
                K.mm(p[:, 128:256], fmT[h][:, 0:128], fmT[h][:, 128:256])
                K.stt(X[h][0][:, 0:128], p[:, 0:128], nbeta[:, h:h + 1], DEC[h][:, 0:128], ALU.mult, ALU.mult)
                K.tt(AQ[h][:], p[:, 128:256], DEC[h][:, 128:256], ALU.mult)
            for h in HS:
                p = K.bank()
                K.tr(p[:, 0:128], X[h][0][:, 0:128], ident[:])
                K.cp(X[h][0][:, 128:256], p[:, 0:128])
            for s in range(7):
                cur = s % 2
                for h in HS:
                    p = K.bank()
                    K.mm(p[:, 0:256], X[h][cur][:, 128:256], Y[h][:])
                    K.tt(Y[h][:], Y[h][:], p[:, 0:256], ALU.add)
                if s < 6:
                    for h in HS:
                        p = K.bank()
                        K.mm(p[:, 0:128], X[h][cur][:, 128:256], X[h][cur][:, 0:128])
                        K.mm(p[:, 128:256], X[h][cur][:, 0:128], X[h][cur][:, 128:256])
                        K.cp(X[h][1 - cur][:], p[:, 0:256])
            wT = [qkv[h][:, 0:128] for h in HS]
            for h in HS:
                p = K.bank()
                K.tr(p[:, 0:128], Y[h][:, 128:256], ident[:])
                K.cp(wT[h][:], p[:, 0:128])
            vn = [qkv[h][:, 128:256] for h in HS]
            for h in HS:
                p = K.bank()
                K.mm(p[:, 0:128], wT[h][:], S_[:, h, :])
                K.tt(vn[h][:], Y[h][:, 0:128], p[:, 0:128], ALU.subtract)
            for h in HS:
                K.mm(PD[h][:, tt * 128:(tt + 1) * 128], S_[:, h, :], fmT[h][:, 256:384], start=True, stop=False)
                K.mm(PD[h][:, tt * 128:(tt + 1) * 128], vn[h][:], AQ[h][:], start=False, stop=True)
            for h in HS:
                p = K.bank()
                K.mm(p[:, 0:128], tm[h][:, 3, :], vn[h][:])
                K.stt(S_[:, h, :], S_[:, h, :], dS[:, h:h + 1], p[:, 0:128], ALU.mult, ALU.add)
        for h in HS:
            head_rms(l, g, PD[h][:, :N], "gdn_norm_w", h, g.gz[:, h, :], g.y[1][:, h, :])
        if blk == NBLK - 1:
            K.dma(o_p_gdn_s[l].rearrange("h k v -> k h v"), S_[:], q="act")

    def gdn_sample_pre(l, g):
        N = g.N
        sq = K.sb("gs_sq", [128, 8, N])
        for j in range(8):
            K.act(sq[:, j, :], g.gs[:, j, :], AF.Square)
        pq = K.bank()
        K.mm(pq[:, 0:8 * N], ones[:], sq[:].rearrange("p a n -> p (a n)"))
        rs = K.sb("gs_rs", [128, 8, N])
        K.act(rs[:].rearrange("p a n -> p (a n)"), pq[:, 0:8 * N], AF.Ln, bias=NEPS)
        K.act(rs[:], rs[:], AF.Exp, scale=-0.5)
        g.qn = K.sb("gs_qn", [128, 4, N])
        g.kn = K.sb("gs_kn", [128, 4, N])
        K.tt(g.qn[:], g.gs[:, 0:4, :], rs[:, 0:4, :], ALU.mult)
        K.ts(g.qn[:], g.qn[:], SCALE, ALU.mult)
        K.tt(g.kn[:], g.gs[:, 4:8, :], rs[:, 4:8, :], ALU.mult)
        pb = K.bank()
        for r_ in range(8):
            K.mm(pb[:, r_ * N:(r_ + 1) * N], sel8[:, r_, :], g.gba[:])
        bc = K.sb("gs_bc", [128, 8, N])
        K.cp(bc[:].rearrange("p a n -> p (a n)"), pb[:, 0:8 * N])
        g.beta = K.sb("gs_beta", [128, 4, N])
        g.eg = K.sb("gs_eg", [128, 4, N])
        K.act(g.beta[:], bc[:, 0:4, :], AF.Exp, scale=-1.0)
        K.ts(g.beta[:], g.beta[:], 1.0, ALU.add)
        K.recip(g.beta[:], g.beta[:])
        for h in range(H):
            K.ts(g.eg[:, h, :], bc[:, 4 + h, :], gbc[l][:, 4 + h:5 + h], ALU.add)
        K.act(g.eg[:], g.eg[:], AF.Exp)
        K.act(g.eg[:], g.eg[:], AF.Ln, bias=1.0)
        for h in range(H):
            K.ts(g.eg[:, h, :], g.eg[:, h, :], gbc[l][:, h:h + 1], ALU.mult)
        K.act(g.eg[:], g.eg[:], AF.Exp)
        g.kb = K.sb("gs_kb", [128, 4, N])
        g.nkn = K.sb("gs_nkn", [128, 4, N])
        K.tt(g.kb[:], g.kn[:], g.beta[:], ALU.mult)
        K.ts(g.nkn[:], g.kn[:], -1.0, ALU.mult)
        g.vtm_g = K.sb("gs_vtm", [NS, W])
        pv = K.bank()
        for c in range(4):
            K.tr(pv[0:NS, c * 128:(c + 1) * 128], g.gs[:, 8 + c, :], ident[:])
        K.cp(g.vtm_g[:], pv[0:NS, :])
        ost = K.sb("gs_ost", [NS, 3 * W])
        for part in range(3):
            pb2 = K.bank()
            for c in range(4):
                K.tr(pb2[0:NS, c * 128:(c + 1) * 128], g.gq[:, part * 4 + c, :, 3], ident[:])
            K.cp(ost[:, part * W:(part + 1) * W], pb2[0:NS, :])
        K.dma(o_s_gdn_conv[l][:, 2, :], ost[:], q="act")
        K.dma(o_s_gdn_conv[l][:, 0:2, :], st_gdn_conv[l][:, 1:3, :], q="act")


    def chunk_core(g, S_, C, qe_b, ke_b, qi, kend_f, v_f, maskfn, dsfn):
        N = g.N
        nch = N // C
        for n in range(nch):
            cs = slice(n * C, (n + 1) * C)
            vk = []
            atm = []
            for h in range(H):
                p = K.bank()
                K.mm(p[0:C, 0:C], ke_b[h][:, cs], qe_b[h][:, cs])
                K.tr(p[0:C, 128:256], v_f[h][:, cs], ident[:])
                K.tr(p[0:C, 256:384], kend_f[h][:, cs], ident[:])
                a = K.sb("cc_at%d_%d" % (h, n % 2), [C, C], BF16)
                K.tt(a[:], p[0:C, 0:C], maskfn(h), ALU.mult)
                v = K.sb("cc_vk%d_%d" % (h, n % 2), [C, 256], BF16)
                K.cp(v[:], p[0:C, 128:384])
                atm.append(a)
                vk.append(v)
            for h in range(H):
                K.mm(PD[h][:, cs], vk[h][:, 0:128], atm[h][:], start=True, stop=False)
                K.mm(PD[h][:, cs], S_[:, h, :], qi[h][:, cs], start=False, stop=True)
            for h in range(H):
                p = K.bank()
                K.mm(p[:, 0:128], vk[h][:, 128:256], vk[h][:, 0:128])
                K.stt(S_[:, h, :], S_[:, h, :], dsfn(h, n), p[:, 0:128], ALU.mult, ALU.add)

    def hgrn_prompt_pre(g):
        N = g.N
        C = HC
        nch = N // C
        qeb, keb, kendf, dS = [], [], [], []
        for h in range(H):
            G = K.sb("hG%d" % (h % 2), [128, N])
            K.act(g.lf[h][:], g.lf[h][:], AF.Ln)
            K.scan(G[:], resetm[:, :N], g.lf[h][:], 0.0, ALU.mult, ALU.add)
            e1 = K.sb("he1_%d" % (h % 2), [128, N])
            K.act(e1[:], G[:], AF.Exp)
            K.tt(g.hq[h][:], g.hq[h][:], e1[:], ALU.mult)
            qb = K.sb("hqb%d" % h, [128, N], BF16)
            K.cp(qb[:], g.hq[h][:])
            K.act(e1[:], G[:], AF.Exp, scale=-1.0)
            kb_ = K.sb("hkb%d" % h, [128, N], BF16)
            K.tt(kb_[:], g.kc[h][:], e1[:], ALU.mult)
            ds_ = K.sb("hdS%d" % h, [128, nch])
            Gv = G[:].rearrange("p (n c) -> p n c", c=C)
            K.act(ds_[:], Gv[:, :, C - 1], AF.Exp)
            K.tt(e1[:].rearrange("p (n c) -> p n c", c=C), Gv, Gv[:, :, C - 1:C].to_broadcast([128, nch, C]), ALU.subtract)
            K.act(e1[:], e1[:], AF.Exp, scale=-1.0)
            K.tt(g.kc[h][:], g.kc[h][:], e1[:], ALU.mult)
            qeb.append(qb)
            keb.append(kb_)
            kendf.append(g.kc[h])
            dS.append(ds_)
        return qeb, keb, kendf, dS

    def branch_hgrn(l, blk, groups):
        with K.scope():
            for g in groups:
                N = g.N
                g.hq = [K.sb("hq%d" % h, [128, N]) for h in range(H)]
                g.kc = [K.sb("hkc%d" % h, [128, N]) for h in range(H)]
                g.lf = [K.sb("hlf%d" % h, [128, N]) for h in range(H)]
                g.hv = [K.sb("hv%d" % h, [128, N]) for h in range(H)]
                g.hgate = K.sb("hgate", [128, 4, N], BF16)

            def ev_q(g, j, p):
                K.act(g.hq[j][:], p, AF.Silu)

            def ev_f(g, j, p):
                f = g.lf[j]
                K.act(f[:], p, AF.Sigmoid)
                K.ts(f[:], f[:], der[l][:, 12 + j:13 + j], ALU.mult, der[l][:, 8 + j:9 + j], ALU.add)
                K.ts(g.kc[j][:], f[:], -1.0, ALU.mult, 1.0, ALU.add)

            def ev_i(g, j, p):
                K.cp(g.hv[j][:], p)

            def ev_g(g, j, p):
                K.act(g.hgate[:, j, :], p, AF.Sigmoid)
            project(l, "hq", groups, ev_q)
            project(l, "hf", groups, ev_f)
            hpre = {}
            for g in groups:
                if g.prompt:
                    hpre[id(g)] = hgrn_prompt_pre(g)
            project(l, "hi", groups, ev_i)
            project(l, "hg", groups, ev_g)
            for g in groups:
                if not g.prompt:
                    g.vtm_h = K.sb("hs_vtm", [NS, W])
                    pv = K.bank()
                    for c in range(4):
                        K.tr(pv[0:NS, c * 128:(c + 1) * 128], g.hv[c][:], ident[:])
                    K.cp(g.vtm_h[:], pv[0:NS, :])
                    gla_sample_step(l, g, st_hgrn_s, o_s_hgrn_s, g.vtm_h,
                                    lambda h, b: g.lf[h][:, b:b + 1], lambda h, b: g.kc[h][:, b:b + 1],
                                    lambda h, b: g.hq[h][:, b:b + 1],
                                    lambda h, po: head_rms(l, g, po, "hgrn_norm_w", h, g.hgate[:, h, :], g.y[2][:, h, :]))
                    continue
                N = g.N
                C = HC
                S_ = Sst[(l, "hgrn")]
                qeb, keb, kendf, dS = hpre[id(g)]
                chunk_core(g, S_, C, qeb, keb, g.hq, kendf, g.hv,
                           lambda h: uincl[0:C, 0:C], lambda h, n: dS[h][:, n:n + 1])
                for h in range(H):
                    head_rms(l, g, PD[h][:, :N], "hgrn_norm_w", h, g.hgate[:, h, :], g.y[2][:, h, :])
                if blk == NBLK - 1:
                    K.dma(o_p_hgrn_s[l].rearrange("h k v -> k h v"), S_[:], q="act")

    def branch_ret(l, blk, groups):
        with K.scope():
            for g in groups:
                N = g.N
                g.rq = [K.sb("rq%d" % h, [128, N]) for h in range(H)]
                g.rk = [K.sb("rk%d" % h, [128, N]) for h in range(H)]
                g.rv = [K.sb("rv%d" % h, [128, N]) for h in range(H)]
                g.rgate = K.sb("rgate", [128, 4, N], BF16)
                if g.prompt:
                    g.cos = K.sb("rcos", [128, N])
                    g.sin = K.sb("rsin", [128, N])
                    K.dma(g.cos[:], c_cos[:, blk * TB:(blk + 1) * TB])
                    K.dma(g.sin[:], c_sin[:, blk * TB:(blk + 1) * TB])

            def ev_q(g, j, p):
                K.cp(g.rq[j][:], p)

            def ev_k(g, j, p):
                K.cp(g.rk[j][:], p)

            def ev_v(g, j, p):
                K.cp(g.rv[j][:], p)

            def ev_g(g, j, p):
                K.act(g.rgate[:, j, :], p, AF.Silu)
            def rotary(g, tl, sc_, j):
                N = g.N
                p = K.bank()
                K.mm(p[:, :N], pswap[:], tl[:])
                t1 = K.sb("rt1_%d" % (j % 2), [128, N])
                t2 = K.sb("rt2_%d" % (j % 2), [128, N])
                if g.prompt:
                    K.stt(t1[:], tl[:], sc_, g.cos[:], ALU.mult, ALU.mult)
                    K.stt(t2[:], p[:, :N], sc_, g.sin[:], ALU.mult, ALU.mult)
                else:
                    K.ts(t1[:], tl[:], cs_s[:, 0:1], ALU.mult, sc_, ALU.mult)
                    K.ts(t2[:], p[:, :N], cs_s[:, 1:2], ALU.mult, sc_, ALU.mult)
                K.tt(tl[:], t1[:], t2[:], ALU.add)

            def ev_q2(g, j, p):
                ev_q(g, j, p)
                rotary(g, g.rq[j], 1.0, j)

            def ev_k2(g, j, p):
                ev_k(g, j, p)
                rotary(g, g.rk[j], SCALE, j)
            project(l, "rq", groups, ev_q2)
            project(l, "rk", groups, ev_k2)
            project(l, "rv", groups, ev_v)
            project(l, "rg", groups, ev_g)
            for g in groups:
                N = g.N
                if not g.prompt:
                    g.vtm_r = K.sb("rs_vtm", [NS, W])
                    pv = K.bank()
                    for c in range(4):
                        K.tr(pv[0:NS, c * 128:(c + 1) * 128], g.rv[c][:], ident[:])
                    K.cp(g.vtm_r[:], pv[0:NS, :])
                    gla_sample_step(l, g, st_ret_s, o_s_ret_s, g.vtm_r,
                                    lambda h, b: GAM[h], lambda h, b: g.rk[h][:, b:b + 1],
                                    lambda h, b: g.rq[h][:, b:b + 1],
                                    lambda h, po: head_gn(l, g, po, h, g.rgate[:, h, :], g.y[3][:, h, :]))
                    continue
                S_ = Sst[(l, "ret")]
                rqb, rkb, qi, kendf = [], [], [], []
                for h in range(H):
                    qb = K.sb("rqb%d" % h, [128, N], BF16)
                    kb_ = K.sb("rkb%d" % h, [128, N], BF16)
                    K.cp(qb[:], g.rq[h][:])
                    K.cp(kb_[:], g.rk[h][:])
                    K.tt(g.rq[h][:].rearrange("p (n c) -> p n c", c=128), g.rq[h][:].rearrange("p (n c) -> p n c", c=128),
                         decq[:, h:h + 1, :].to_broadcast([128, N // 128, 128]), ALU.mult)
                    K.tt(g.rk[h][:].rearrange("p (n c) -> p n c", c=128), g.rk[h][:].rearrange("p (n c) -> p n c", c=128),
                         dendr[:, h:h + 1, :].to_broadcast([128, N // 128, 128]), ALU.mult)
                    rqb.append(qb)
                    rkb.append(kb_)
                chunk_core(g, S_, 128, rqb, rkb, g.rq, g.rk, g.rv,
                           lambda h: dmask[:, h, :], lambda h, n: GAM[h] ** 128)
                for h in range(H):
                    head_gn(l, g, PD[h][:, :N], h, g.rgate[:, h, :], g.y[3][:, h, :])
                if blk == NBLK - 1:
                    K.dma(o_p_ret_s[l].rearrange("h k v -> k h v"), S_[:], q="act")

    def branch_xatt(l, blk, groups):
        with K.scope():
            for g in groups:
                g.cq = K.sb("cq", [128, 4, g.N], BF16 if g.prompt else F32)

            def ev(g, j, p):
                K.cp(g.cq[:, j, :], p)
            project(l, "xq", groups, ev)
            for g in groups:
                N = g.N
                if not g.prompt:
                    g.qtm = K.sb("xs_qtm", [NS, W])
                    pv = K.bank()
                    for c in range(4):
                        K.tr(pv[0:NS, c * 128:(c + 1) * 128], g.cq[:, c, :], ident[:])
                    K.cp(g.qtm[:], pv[0:NS, :])
                    xatt_sample_step(l, g)
                    continue
                for h in range(H):
                    E_ = K.sb("xE%d" % (h % 2), [128, 2, N], BF16)
                    for mc in range(2):
                        p = K.bank()
                        K.mm(p[:, :N], memKT[l][:, h, mc * 128:(mc + 1) * 128], g.cq[:, h, :])
                        K.act(E_[:, mc, :], p[:, :N], AF.Exp, scale=SCALE)
                    po = K.bank()
                    pd = K.bank()
                    for mc in range(2):
                        K.mm(po[:, :N], memV[l][:, mc, h * 128:(h + 1) * 128], E_[:, mc, :], start=(mc == 0), stop=(mc == 1))
                    for mc in range(2):
                        K.mm(pd[:, :N], onesb[:], E_[:, mc, :], start=(mc == 0), stop=(mc == 1))
                    rd = K.sb("xrd", [128, N])
                    K.recip(rd[:], pd[:, :N])
                    K.tt(g.y[4][:, h, :], po[:, :N], rd[:], ALU.mult)


    def selb_for(b):
        t = K.sb("selb%d" % (b % 4), [NS, 128])
        K.ts(t[:], ones[0:NS, :], ident[0:NS, b:b + 1], ALU.mult)
        return t

    def gdn_sample_step(l, g):
        po = PD[0]
        for b in range(NS):
            sel = selb_for(b)
            S_in = K.sb("ss_in%d" % (b % 4), [128, H, 128])
            K.dma(S_in[:], st_gdn_s[l, b].rearrange("h k v -> k h v"))
            Sp = K.sb("ss_p%d" % (b % 2), [128, H, 128])
            Sn = K.sb("ss_n%d" % (b % 4), [128, H, 128])
            nk = K.sb("ss_nk%d" % (b % 2), [128, H, 128])
            for h in range(H):
                K.ts(Sp[:, h, :], S_in[:, h, :], g.eg[:, h, b:b + 1], ALU.mult)
                K.ts(nk[:, h, :], ones[:], g.nkn[:, h, b:b + 1], ALU.mult)
            ps_ = []
            for h in range(H):
                hs = slice(h * 128, (h + 1) * 128)
                p1 = K.bank()
                K.mm(p1[:, 0:128], sel[:], g.vtm_g[:, hs], start=True, stop=False)
                K.mm(p1[:, 0:128], nk[:, h, :], Sp[:, h, :], start=False, stop=True)
                ps_.append(p1)
            for h in range(H):
                K.stt(Sn[:, h, :], ps_[h][:, 0:128], g.kb[:, h, b:b + 1], Sp[:, h, :], ALU.mult, ALU.add)
            for h in range(H):
                K.mm(po[:, h * NS + b:h * NS + b + 1], Sn[:, h, :], g.qn[:, h, b:b + 1])
            K.dma(o_s_gdn_s[l, b].rearrange("h k v -> k h v"), Sn[:], q="act")
        for h in range(H):
            head_rms(l, g, po[:, h * NS:(h + 1) * NS], "gdn_norm_w", h, g.gz[:, h, :], g.y[1][:, h, :])

    def gla_sample_step(l, g, st_in, st_out, vtm, fcol, kcol, qcol, post):
        po = PD[0]
        for b in range(NS):
            sel = selb_for(b)
            S_in = K.sb("ss_in%d" % (b % 2), [128, H, 128])
            K.dma(S_in[:], st_in[l, b].rearrange("h k v -> k h v"))
            Sn = K.sb("ss_n%d" % (b % 2), [128, H, 128])
            pv = K.bank()
            K.mm(pv[:, 0:W], sel[:], vtm[:])
            for h in range(H):
                K.ts(Sn[:, h, :], S_in[:, h, :], fcol(h, b), ALU.mult)
            for h in range(H):
                hs = slice(h * 128, (h + 1) * 128)
                K.stt(Sn[:, h, :], pv[:, hs], kcol(h, b), Sn[:, h, :], ALU.mult, ALU.add)
            for h in range(H):
                K.mm(po[:, h * NS + b:h * NS + b + 1], Sn[:, h, :], qcol(h, b))
            K.dma(st_out[l, b].rearrange("h k v -> k h v"), Sn[:], q="act")
        for h in range(H):
            post(h, po[:, h * NS:(h + 1) * NS])

    def xatt_sample_step(l, g):
        po = PD[0]
        E_all = K.sb("xs_E", [128, 2, NS * H])
        sT = K.sb("xs_sT", [128, 2, NS * H])
        for b in range(NS):
            sel = selb_for(b)
            Kc = K.sb("xs_K%d" % (b % 2), [128, 2, W])
            Vc = K.sb("xs_V%d" % (b % 2), [128, 2, W])
            K.dma(Kc[:], cache_k[l, b].rearrange("(mc m) c -> m mc c", m=128))
            K.dma(Vc[:], cache_v[l, b].rearrange("(mc m) c -> m mc c", m=128))
            pq = K.bank()
            K.mm(pq[:, 0:W], sel[:], g.qtm[:])
            qbs = K.sb("xs_qb", [128, W])
            K.cp(qbs[:], pq[:, 0:W])
            for mc in range(2):
                prod = K.sb("xs_prod", [128, W])
                K.tt(prod[:], Kc[:, mc, :], qbs[:], ALU.mult)
                K.rsum(sT[:, mc, b * 4:(b + 1) * 4], prod[:].rearrange("p (h d) -> p h d", h=H))
            K.act(E_all[:, :, b * 4:(b + 1) * 4], sT[:, :, b * 4:(b + 1) * 4], AF.Exp, scale=SCALE)
            for h in range(H):
                for mc in range(2):
                    K.mm(po[:, b * 4 + h:b * 4 + h + 1], Vc[:, mc, h * 128:(h + 1) * 128], E_all[:, mc, b * 4 + h:b * 4 + h + 1],
                         start=(mc == 0), stop=(mc == 1))
        pd = K.bank()
        for mc in range(2):
            K.mm(pd[:, 0:NS * H], ones[:], E_all[:, mc, :], start=(mc == 0), stop=(mc == 1))
        rd = K.sb("xs_rd", [128, NS * H])
        K.recip(rd[:], pd[:, 0:NS * H])
        for h in range(H):
            K.tt(g.y[4][:, h, :], po[:, 0:NS * H].rearrange("p (b h) -> p h b", h=H)[:, h, :],
                 rd[:].rearrange("p (b h) -> p h b", h=H)[:, h, :], ALU.mult)

    def merge(l, groups):
        with K.scope():
            K.n_rot = 8
            for g in groups:
                g.mg = K.sb("mg", [128, 8, g.N])
            for n in range(5):
                wb = wload(w_branch[l, n], 4, D)
                for half in range(2):
                    wg = wload(w_merge_gate[l, n][:, half * 512:(half + 1) * 512], 8, 512)
                    for g in groups:
                        N = g.N
                        for jj in range(4):
                            j = half * 4 + jj
                            pg = K.bank()
                            pz = K.bank()
                            for k in range(8):
                                K.mm(pg[:, :N], wg[:, k, jj * 128:(jj + 1) * 128], g.xb[:, k, :N], start=(k == 0), stop=(k == 7))
                            for k in range(4):
                                K.mm(pz[:, :N], wb[:, k, j * 128:(j + 1) * 128], g.y[n][:, k, :], start=(k == 0), stop=(k == 3))
                            gt = K.sb("gt%d" % (jj % 2), [128, N])
                            K.act(gt[:], pg[:, :N], AF.Sigmoid, bias=pcol(l, "b_merge_gate", n * 8 + j))
                            if n == 0:
                                K.tt(g.mg[:, j, :], gt[:], pz[:, :N], ALU.mult)
                            else:
                                K.tt(gt[:], gt[:], pz[:, :N], ALU.mult)
                                K.tt(g.mg[:, j, :], g.mg[:, j, :], gt[:], ALU.add)
            for g in groups:
                g.mb = K.sb("mb", [128, 8, g.N], BF16)
                g.z = K.sb("z1", [128, 8, g.N])
                for c in range(8):
                    K.cp(g.mb[:, c, :], g.mg[:, c, :])
            K.n_rot = 4
            K.rr = 0
            for g in groups:
                g.lnst = (K.ps[6][:, :g.N], K.ps[7][:, :g.N]) if g.prompt else (K.ps[4][:, 0:NS], K.ps[5][:, 0:NS])
            for half in range(2):
                wo = wload(w_out[l][:, half * 512:(half + 1) * 512], 8, 512)
                for g in groups:
                    N = g.N
                    for jj in range(4):
                        j = half * 4 + jj
                        p = K.bank()
                        for k in range(8):
                            K.mm(p[:, :N], wo[:, k, jj * 128:(jj + 1) * 128], g.mb[:, k, :], start=(k == 0), stop=(k == 7))
                        K.stt(g.z[:, j, :], g.xf[:, j, :N], ALPHA, p[:, :N], ALU.mult, ALU.add)
                        ln_stat_step(g, g.z, j, g.lnst[0], g.lnst[1])
            for g in groups:
                layernorm(l, g, g.z, "ln1_g", "ln1_b", stats=g.lnst)
            K.n_rot = 4
            K.rr = 0

    def ffn(l, groups):
        with K.scope():
            for g in groups:
                g.fa = K.sb("ffa", [128, 22, g.N], BF16)
                g.z = K.sb("z2", [128, 8, g.N])
            K.n_rot = 8
            for grp in range(6):
                j0 = grp * 4
                nj = min(4, 22 - j0)
                wgt = wload(w_ffn_up[l][:, j0 * 128:(j0 + nj) * 128], 8, nj * 128)
                wvt = wload(w_ffn_up[l][:, DFF + j0 * 128:DFF + (j0 + nj) * 128], 8, nj * 128)
                for g in groups:
                    N = g.N
                    for jj in range(nj):
                        pg = K.bank()
                        pv = K.bank()
                        for k in range(8):
                            K.mm(pg[:, :N], wgt[:, k, jj * 128:(jj + 1) * 128], g.xb[:, k, :N], start=(k == 0), stop=(k == 7))
                        for k in range(8):
                            K.mm(pv[:, :N], wvt[:, k, jj * 128:(jj + 1) * 128], g.xb[:, k, :N], start=(k == 0), stop=(k == 7))
                        sg = K.sb("fsg%d" % (jj % 2), [128, N])
                        K.act(sg[:], pg[:, :N], AF.Silu)
                        K.tt(g.fa[:, j0 + jj, :], sg[:], pv[:, :N], ALU.mult)
            K.n_rot = 4
            K.rr = 0
            for half in range(2):
                for kg in range(3):
                    nk = 8 if kg < 2 else 6
                    wd = wload(w_ffn_down[l][kg * 1024:kg * 1024 + nk * 128, half * 512:(half + 1) * 512], nk, 512)
                    for g in groups:
                        N = g.N
                        for jj in range(4):
                            acc = PD[jj][:, :N] if g.prompt else K.ps[jj][:, :N]
                            for kk in range(nk):
                                K.mm(acc, wd[:, kk, jj * 128:(jj + 1) * 128], g.fa[:, kg * 8 + kk, :],
                                     start=(kg == 0 and kk == 0), stop=(kg == 2 and kk == nk - 1))
                inter = len(groups) == 1
                for g in groups:
                    N = g.N
                    g.lnst = (K.ps[1][:, :N], K.ps[2][:, :N]) if inter else None
                    for jj in range(4):
                        acc = PD[jj][:, :N] if g.prompt else K.ps[jj][:, :N]
                        K.stt(g.z[:, half * 4 + jj, :], g.xf[:, half * 4 + jj, :N], ALPHA, acc, ALU.mult, ALU.add)
                        if inter:
                            ln_stat_step(g, g.z, half * 4 + jj, g.lnst[0], g.lnst[1])
            for g in groups:
                layernorm(l, g, g.z, "ln2_g", "ln2_b", stats=g.lnst)

    def mem_kv(l):
        with K.scope():
            if l == 0:
                for mt in range(2):
                    xin = K.sb("xmin", [128, D])
                    K.dma(xin[:], mem_prompt[mt * 128:(mt + 1) * 128, :])
                    for hf in range(2):
                        p = K.bank()
                        for c in range(4):
                            K.tr(p[:, c * 128:(c + 1) * 128], xin[:, (hf * 4 + c) * 128:(hf * 4 + c + 1) * 128], ident[:])
                        K.cp(xmT[:, hf * 4:hf * 4 + 4, mt * 128:(mt + 1) * 128], p[:].rearrange("p (c t) -> p c t", c=4))
            wk = wload(w_mem_k[l], 8, W, cache=False)
            for h in range(H):
                p = K.bank()
                for k in range(8):
                    K.mm(p[:, 0:NMEM], wk[:, k, h * 128:(h + 1) * 128], xmT[:, k, :], start=(k == 0), stop=(k == 7))
                K.cp(memKT[l][:, h, :], p[:, 0:NMEM])
            for mc in range(2):
                p = K.bank()
                for k in range(8):
                    K.mm(p[:, 0:W], xmT[:, k, mc * 128:(mc + 1) * 128], wk[:, k, :], start=(k == 0), stop=(k == 7))
                st = K.sb("mkst%d" % mc, [128, W])
                K.cp(st[:], p[:, 0:W])
                K.dma(o_p_mem_k[l][mc * 128:(mc + 1) * 128, :], st[:], q="act")
            wv = wload(w_mem_v[l], 8, W, cache=False)
            for mc in range(2):
                p = K.bank()
                for k in range(8):
                    K.mm(p[:, 0:W], xmT[:, k, mc * 128:(mc + 1) * 128], wv[:, k, :], start=(k == 0), stop=(k == 7))
                st = K.sb("mvst%d" % mc, [128, W])
                K.cp(st[:], p[:, 0:W])
                K.cp(memV[l][:, mc, :], p[:, 0:W], eng="dve")
                K.dma(o_p_mem_v[l][mc * 128:(mc + 1) * 128, :], st[:], q="act")

    import os as _os
    DBGX = int(_os.environ.get("DBGX", "0"))

    def load_x(blk):
        with K.scope():
            for tt in range(int(_os.environ.get("DBGT", "4"))):
                xin = K.sb("xin%d" % (tt % 2), [128, D])
                r0 = blk * TB + tt * 128
                K.dma(xin[:], x_prompt[r0:r0 + 128, :])
                for hf in range(2):
                    p = K.bank()
                    for c in range(4):
                        K.tr(p[:, c * 128:(c + 1) * 128], xin[:, (hf * 4 + c) * 128:(hf * 4 + c + 1) * 128], ident[:])
                    K.cp(P.xf[:, hf * 4:hf * 4 + 4, tt * 128:(tt + 1) * 128], p[:].rearrange("p (c t) -> p c t", c=4))
                    if not (DBGX & 2):
                        if DBGX & 8:
                            o_ = P.xb[:, hf * 4:hf * 4 + 4, tt * 128:(tt + 1) * 128]
                            i_ = p[:].rearrange("p (c t) -> p c t", c=4)
                            K.S.op("dve", lambda e: e.tensor_copy(o_, i_), _keys(p[:]) + _keys(P.xf[:]), _keys(P.xb[:]) + _keys(p[:]))
                        else:
                            K.cp(P.xb[:, hf * 4:hf * 4 + 4, tt * 128:(tt + 1) * 128], p[:].rearrange("p (c t) -> p c t", c=4), eng="dve")
            if blk == 0 and not (DBGX & 1):
                xs = K.sb("xins", [NS, D])
                K.dma(xs[:], x_sample)
                p = K.bank()
                for c in range(8):
                    K.tr(p[:, c * NS:(c + 1) * NS], xs[:, c * 128:(c + 1) * 128], ident[0:NS, 0:NS])
                K.cp(Sg.xf[:], p[:, 0:8 * NS].rearrange("p (c t) -> p c t", c=8))
                K.cp(Sg.xb[:], p[:, 0:8 * NS].rearrange("p (c t) -> p c t", c=8), eng="dve")

    def store_y(blk):
        with K.scope():
            for tt in range(4):
                yst = K.sb("yst%d" % (tt % 2), [128, D])
                for hf in range(2):
                    p = K.bank()
                    for c in range(4):
                        K.tr(p[:, c * 128:(c + 1) * 128], P.xf[:, hf * 4 + c, tt * 128:(tt + 1) * 128], ident[:])
                    K.cp(yst[:, hf * 512:(hf + 1) * 512], p[:])
                r0 = blk * TB + tt * 128
                K.dma(y_prompt[r0:r0 + 128, :], yst[:], q="act")
            if blk == 0:
                ys = K.sb("ysts", [NS, D])
                for hf in range(2):
                    p = K.bank()
                    for c in range(4):
                        K.tr(p[0:NS, c * 128:(c + 1) * 128], Sg.xf[:, hf * 4 + c, :], ident[:])
                    K.cp(ys[:, hf * 512:(hf + 1) * 512], p[0:NS, :])
                K.dma(y_sample, ys[:], q="act")

    import os
    kstop = int(os.environ.get("KSTOP", "1000000"))
    st_ = {"n": 0}

    class _Stop(Exception):
        pass

    def stage(tag):
        st_["n"] += 1
        STAGELOG.append((st_["n"], tag, dict(K.S.cnt)))
        if st_["n"] >= kstop:
            print("STOP at stage", st_["n"], tag)
            raise _Stop()

    try:
        stage("prep")
        for blk in range(NBLK):
            load_x(blk)
            stage("load_x")
            for l in range(L):
                groups = [P] + ([Sg] if blk == 0 else [])
                if blk == 0:
                    mem_kv(l)
                    stage("mem_kv")
                cur["blk"], cur["l"], cur["widx"] = blk, l, 0
                with K.scope():
                    for g in groups:
                        g.y = [K.sb("y%d" % n, [128, 4, g.N], BF16) for n in range(5)]
                    branch_lru(l, blk, groups)
                    stage("lru")
                    branch_gdn(l, blk, groups)
                    stage("gdn")
                    branch_hgrn(l, blk, groups)
                    stage("hgrn")
                    branch_ret(l, blk, groups)
                    stage("ret")
                    branch_xatt(l, blk, groups)
                    stage("xatt")
                    merge(l, groups)
                    stage("merge")
                ffn(l, groups)
                wflush()
                stage("ffn")
            store_y(blk)
            stage("store")
    except _Stop:
        pass
    K.S.finish("sp")
    K.stack.pop()
    root.close()
    return nc, K


_CONST_CACHE = {}
STAGELOG = []


def _consts():
    if _CONST_CACHE:
        return _CONST_CACHE
    f = np.float32
    idx = np.arange(128)
    c = {}
    c["c_ident"] = np.eye(128, dtype=f)
    c["c_uincl"] = (idx[:, None] <= idx[None, :]).astype(f)
    c["c_ugt"] = (idx[:, None] > idx[None, :]).astype(f)
    c["c_pswap"] = (idx[:, None] == (idx[None, :] + 64) % 128).astype(f)
    c["c_posm"] = np.where(idx[None, :] < idx[:, None], 0.0, BIG).astype(f)
    c["c_negm"] = np.where(idx[:, None] <= idx[None, :], 0.0, -BIG).astype(f)
    dm = np.zeros((128, H, 128), f)
    dq = np.zeros((128, H, 128), f)
    de = np.zeros((128, H, 128), f)
    for h in range(H):
        gam = np.float64(GAM[h])
        diff = idx[None, :] - idx[:, None]
        dm[:, h, :] = np.where(diff >= 0, gam ** np.maximum(diff, 0), 0.0)
        dq[:, h, :] = (gam ** (idx + 1))[None, :]
        de[:, h, :] = (gam ** (127 - idx))[None, :]
    c["c_dmask"] = dm.reshape(128, H * 128)
    c["c_decq"] = dq.reshape(128, H * 128)
    c["c_dendr"] = de.reshape(128, H * 128)
    half = 64
    inv = np.power(f(10000.0), -(np.arange(half, dtype=f) / f(half))).astype(f)
    pos = np.arange(T, dtype=f)
    ang = (pos[:, None] * inv[None, :]).astype(f)
    cos = np.cos(ang).astype(f).T
    sin = np.sin(ang).astype(f).T
    c["c_cos"] = np.ascontiguousarray(np.concatenate([cos, cos], 0))
    c["c_sin"] = np.ascontiguousarray(np.concatenate([-sin, sin], 0))
    angs = (f(PAST) * inv).astype(f)
    cs = np.zeros((128, 2), f)
    cs[:, 0] = np.concatenate([np.cos(angs), np.cos(angs)])
    cs[:, 1] = np.concatenate([-np.sin(angs), np.sin(angs)])
    c["c_cs_s"] = cs
    rm = np.ones((128, TB), f)
    rm[:, ::HC] = 0.0
    c["c_reset"] = rm
    _CONST_CACHE.update(c)
    return _CONST_CACHE


_PROG = {}


def kernel(**inp):
    f = np.float32
    if "nc" not in _PROG:
        _PROG["nc"], _ = build_program()
    nc = _PROG["nc"]
    consts = _consts()
    shared = {k: np.ascontiguousarray(inp[k], dtype=f) for k in (
        "w_in", "lru_conv_w", "lru_conv_b", "lru_wa", "lru_ba", "lru_wi", "lru_bi", "lru_lambda", "gdn_conv_w",
        "gdn_a_log", "gdn_dt_bias", "gdn_norm_w", "hgrn_lb_raw", "hgrn_norm_w", "ret_gn_w", "ret_gn_b",
        "w_mem_k", "w_mem_v", "w_merge_gate", "b_merge_gate", "w_branch", "w_out", "ln1_g", "ln1_b",
        "w_ffn_up", "w_ffn_down", "ln2_g", "ln2_b")}
    in_maps = []
    for c in range(NCORE):
        sl = slice(c * NS, (c + 1) * NS)
        m = dict(shared)
        m.update(consts)
        m["x_prompt"] = np.ascontiguousarray(inp["x_prompt"][c], dtype=f)
        m["x_sample"] = np.ascontiguousarray(inp["x_sample"][sl, 0, :], dtype=f)
        m["mem_prompt"] = np.ascontiguousarray(inp["mem_prompt"][c], dtype=f)
        m["state_lru_h"] = np.ascontiguousarray(inp["state_lru_h"][:, sl], dtype=f)
        m["state_lru_conv"] = np.ascontiguousarray(inp["state_lru_conv"][:, sl], dtype=f)
        m["state_gdn_conv"] = np.ascontiguousarray(inp["state_gdn_conv"][:, sl], dtype=f)
        m["state_gdn_s"] = np.ascontiguousarray(inp["state_gdn_s"][:, sl], dtype=f)
        m["state_hgrn_s"] = np.ascontiguousarray(inp["state_hgrn_s"][:, sl], dtype=f)
        m["state_ret_s"] = np.ascontiguousarray(inp["state_ret_s"][:, sl], dtype=f)
        m["cache_mem_k"] = np.ascontiguousarray(inp["cache_mem_k"][:, sl].reshape(L, NS, NMEM, W), dtype=f)
        m["cache_mem_v"] = np.ascontiguousarray(inp["cache_mem_v"][:, sl].reshape(L, NS, NMEM, W), dtype=f)
        in_maps.append(m)
    res = run_bass_kernel_spmd(nc, in_maps, core_ids=list(range(NCORE)))
    R = res.results

    def cat1(name, axis):
        return np.concatenate([np.asarray(R[c][name], dtype=f) for c in range(NCORE)], axis=axis)

    def stack1(name):
        return np.stack([np.asarray(R[c][name], dtype=f) for c in range(NCORE)], axis=1)
    y_p = np.stack([np.asarray(R[c]["y_prompt"], dtype=f) for c in range(NCORE)], axis=0)
    y_s = cat1("y_sample", 0).reshape(NCORE * NS, 1, D)
    return (y_p, y_s,
            stack1("p_lru_h"), stack1("p_lru_conv"), stack1("p_gdn_conv"),
            stack1("p_gdn_s"), stack1("p_hgrn_s"), stack1("p_ret_s"),
            stack1("p_mem_k").reshape(L, NCORE, NMEM, H, HD), stack1("p_mem_v").reshape(L, NCORE, NMEM, H, HD),
            cat1("s_lru_h", 1), cat1("s_lru_conv", 1), cat1("s_gdn_conv", 1),
            cat1("s_gdn_s", 1), cat1("s_hgrn_s", 1), cat1("s_ret_s", 1))
```
